# Optimizing a Trainium2 kernel written in Bass

```python
import math
import jax, jax.numpy as jnp
from jax import lax
import numpy as np

D_MODEL = 1024
BATCH = 2
SEQ = 16384
DEPTH = 2

D_MIX = 2 * D_MODEL
M_HEADS = 4
M_HEAD_DIM = D_MIX // 4 // M_HEADS
M_WIDTH = M_HEADS * M_HEAD_DIM
M_CONV = 4
M_CHUNK = 128
S_WIDTH = D_MIX // 2
S_HEAD_DIM = 64
S_HEADS = S_WIDTH // S_HEAD_DIM
S_GROUPS = 4
S_STATE = 128
S_CONV = 4
S_CHUNK = 128
C_WIDTH = D_MIX - M_WIDTH - S_WIDTH
C_KERNEL = 31
D_FF = ((8 * D_MODEL // 3 + 255) // 256) * 256

RMS_EPS = 1e-6
LN_EPS = 1e-5
M_INIT_LOG = -1e30

IN_SIZES = (M_WIDTH, M_WIDTH, M_WIDTH, M_WIDTH, M_HEADS, M_HEADS,
            S_WIDTH, S_WIDTH + 2 * S_GROUPS * S_STATE, S_HEADS, 2 * C_WIDTH)
D_IN_PROJ = sum(IN_SIZES)

kernel_name = "hybrid_mlstm_ssd_convmodule_macaron"


def rmsnorm(x, g):
    xf = x.astype(jnp.float32)
    y = xf * lax.rsqrt(jnp.mean(xf * xf, axis=-1, keepdims=True) + RMS_EPS)
    return (y * g.astype(jnp.float32)).astype(x.dtype)


def group_rmsnorm(x, g, n_groups):
    shp = x.shape
    xf = x.astype(jnp.float32).reshape(shp[:-1] + (n_groups, shp[-1] // n_groups))
    y = xf * lax.rsqrt(jnp.mean(xf * xf, axis=-1, keepdims=True) + RMS_EPS)
    return (y.reshape(shp) * g.astype(jnp.float32)).astype(x.dtype)


def layernorm(x, g, b):
    xf = x.astype(jnp.float32)
    mu = jnp.mean(xf, axis=-1, keepdims=True)
    xc = xf - mu
    y = xc * lax.rsqrt(jnp.mean(xc * xc, axis=-1, keepdims=True) + LN_EPS)
    return (y * g.astype(jnp.float32) + b.astype(jnp.float32)).astype(x.dtype)


def causal_dwconv(x, w, b):
    width = w.shape[0]
    y = lax.conv_general_dilated(
        x, w[:, None, :].astype(x.dtype), window_strides=(1,),
        padding=[(width - 1, 0)], dimension_numbers=("NWC", "WIO", "NWC"),
        feature_group_count=x.shape[-1])
    return y + b.astype(x.dtype)


def swiglu(h, wg, wu, wd):
    return (jax.nn.silu(h @ wg) * (h @ wu)) @ wd


def split_cols(a, sizes):
    out, off = [], 0
    for s in sizes:
        out.append(a[..., off:off + s])
        off += s
    return out


def mlstm_chunkwise(q, k, v, i_pre, f_pre):
    bsz, s, nh, dh = q.shape
    L = M_CHUNK
    nc = s // L
    f32 = jnp.float32

    def chunks(a):
        return a.astype(f32).reshape(bsz, nc, L, nh, -1).transpose(1, 0, 3, 2, 4)

    qc = chunks(q)
    kc = chunks(k) * (dh ** -0.5)
    vc = chunks(v)
    ic = chunks(i_pre[..., None])[..., 0]
    fc = chunks(jax.nn.log_sigmoid(f_pre.astype(f32))[..., None])[..., 0]
    causal = jnp.tril(jnp.ones((L, L), dtype=bool))

    def step(carry, inp):
        C, n, m = carry
        qb, kb, vb, ib, fb = inp
        b = jnp.cumsum(fb, axis=-1)
        g = b[..., -1]
        dmat = jnp.where(causal, b[..., :, None] - b[..., None, :] + ib[..., None, :], -jnp.inf)
        inter = b + m[..., None]
        m_t = jnp.maximum(inter, jnp.max(dmat, axis=-1))
        w = jnp.exp(dmat - m_t[..., None])
        sqk = jnp.einsum("bhtd,bhsd->bhts", qb, kb) * w
        a = jnp.exp(inter - m_t)
        num = jnp.einsum("bhts,bhse->bhte", sqk, vb) + a[..., None] * jnp.einsum("bhed,bhtd->bhte", C, qb)
        den = jnp.sum(sqk, axis=-1) + a * jnp.einsum("bhd,bhtd->bht", n, qb)
        h = num / jnp.maximum(jnp.abs(den), jnp.exp(-m_t))[..., None]
        u = g[..., None] - b + ib
        m_new = jnp.maximum(g + m, jnp.max(u, axis=-1))
        ws = jnp.exp(u - m_new[..., None])
        decay = jnp.exp(g + m - m_new)
        C_new = decay[..., None, None] * C + jnp.einsum("bhs,bhse,bhsd->bhed", ws, vb, kb)
        n_new = decay[..., None] * n + jnp.einsum("bhs,bhsd->bhd", ws, kb)
        return (C_new, n_new, m_new), h

    init = (jnp.zeros((bsz, nh, dh, dh), f32), jnp.zeros((bsz, nh, dh), f32),
            jnp.full((bsz, nh), M_INIT_LOG, f32))
    _, hs = lax.scan(step, init, (qc, kc, vc, ic, fc))
    return hs.transpose(1, 0, 3, 2, 4).reshape(bsz, s, nh, dh)


def ssd_chunkwise(xs, dt, A, Bm, Cm):
    bsz, s, nh, hp = xs.shape
    ng, ns = Bm.shape[2], Bm.shape[3]
    r = nh // ng
    L = S_CHUNK
    nc = s // L
    f32 = jnp.float32
    X = xs.astype(f32) * dt[..., None]
    adt = dt * A
    xc = X.reshape(bsz, nc, L, ng, r, hp).transpose(1, 0, 3, 4, 2, 5)
    ac = adt.reshape(bsz, nc, L, ng, r).transpose(1, 0, 3, 4, 2)
    bc = Bm.astype(f32).reshape(bsz, nc, L, ng, ns).transpose(1, 0, 3, 2, 4)
    cc = Cm.astype(f32).reshape(bsz, nc, L, ng, ns).transpose(1, 0, 3, 2, 4)
    causal = jnp.tril(jnp.ones((L, L), dtype=bool))

    def step(hstate, inp):
        xb, ab, bb, cb = inp
        cs = jnp.cumsum(ab, axis=-1)
        lmat = jnp.exp(jnp.where(causal, cs[..., :, None] - cs[..., None, :], -jnp.inf))
        cbm = jnp.einsum("bgtn,bgsn->bgts", cb, bb)
        y_diag = jnp.einsum("bgrts,bgrsp->bgrtp", cbm[:, :, None] * lmat, xb)
        y_off = jnp.einsum("bgtn,bgrpn->bgrtp", cb, hstate) * jnp.exp(cs)[..., None]
        decay_s = jnp.exp(cs[..., -1:] - cs)
        h_new = jnp.exp(cs[..., -1])[..., None, None] * hstate + \
            jnp.einsum("bgsn,bgrs,bgrsp->bgrpn", bb, decay_s, xb)
        return h_new, y_diag + y_off

    init = jnp.zeros((bsz, ng, r, hp, ns), f32)
    _, ys = lax.scan(step, init, (xc, ac, bc, cc))
    return ys.transpose(1, 0, 4, 2, 3, 5).reshape(bsz, s, nh, hp)


def hybrid_mixer(h, w_in, w_out, mlstm_conv_w, mlstm_conv_b, mlstm_gate_b, mlstm_norm,
                 ssd_conv_w, ssd_conv_b, ssd_dt_bias, ssd_a_log, ssd_d, ssd_norm,
                 cm_conv_w, cm_conv_b, cm_ln_g, cm_ln_b):
    bsz, s, _ = h.shape
    f32 = jnp.float32
    proj = h @ w_in
    q, k, v, o, i_pre, f_pre, z, xbc, dt, glu = split_cols(proj, IN_SIZES)

    qk = jax.nn.silu(causal_dwconv(jnp.concatenate([q, k], axis=-1), mlstm_conv_w, mlstm_conv_b))
    q, k = qk[..., :M_WIDTH], qk[..., M_WIDTH:]
    heads = lambda a: a.reshape(bsz, s, M_HEADS, M_HEAD_DIM)
    hm = mlstm_chunkwise(heads(q), heads(k), heads(v),
                         i_pre + mlstm_gate_b[:M_HEADS], f_pre + mlstm_gate_b[M_HEADS:])
    hm = jax.nn.sigmoid(o) * hm.reshape(bsz, s, M_WIDTH).astype(h.dtype)
    hm = group_rmsnorm(hm, mlstm_norm, M_HEADS)

    xbc = jax.nn.silu(causal_dwconv(xbc, ssd_conv_w, ssd_conv_b))
    gn = S_GROUPS * S_STATE
    xs, Bm, Cm = xbc[..., :S_WIDTH], xbc[..., S_WIDTH:S_WIDTH + gn], xbc[..., S_WIDTH + gn:]
    dtp = jax.nn.softplus((dt + ssd_dt_bias).astype(f32))
    A = -jnp.exp(ssd_a_log.astype(f32))
    xs_h = xs.reshape(bsz, s, S_HEADS, S_HEAD_DIM)
    y = ssd_chunkwise(xs_h, dtp, A,
                      Bm.reshape(bsz, s, S_GROUPS, S_STATE), Cm.reshape(bsz, s, S_GROUPS, S_STATE))
    y = y + ssd_d.astype(f32)[:, None] * xs_h.astype(f32)
    y = y.reshape(bsz, s, S_WIDTH).astype(h.dtype) * jax.nn.silu(z)
    y = group_rmsnorm(y, ssd_norm, S_GROUPS)

    u = glu[..., :C_WIDTH] * jax.nn.sigmoid(glu[..., C_WIDTH:])
    u = causal_dwconv(u, cm_conv_w, cm_conv_b)
    u = jax.nn.silu(layernorm(u, cm_ln_g, cm_ln_b))

    return jnp.concatenate([hm, y, u], axis=-1) @ w_out


def setup_inputs(seed: int = 0) -> dict:
    key = jax.random.key(seed)
    ks = jax.random.split(key, 27)
    f32 = jnp.float32
    nrm = lambda kk, shape, scale: jax.random.normal(kk, shape, f32) * scale
    gain = lambda kk, n: 1.0 + nrm(kk, (DEPTH, n), 0.02)

    x = nrm(ks[0], (BATCH, SEQ, D_MODEL), 1.0)

    gate_noise = nrm(ks[10], (DEPTH, 2 * M_HEADS), 0.1)
    gate_base = jnp.concatenate([jnp.zeros((M_HEADS,), f32), jnp.linspace(3.0, 6.0, M_HEADS, dtype=f32)])
    dt0 = jnp.exp(jax.random.uniform(ks[14], (DEPTH, S_HEADS), f32, math.log(1e-3), math.log(1e-1)))
    dt_bias = dt0 + jnp.log(-jnp.expm1(-dt0))
    a_log = jnp.log(jax.random.uniform(ks[15], (DEPTH, S_HEADS), f32, 1.0, 16.0))

    return {
        "x": x,
        "ffn1_norm": gain(ks[1], D_MODEL),
        "ffn1_w_gate": nrm(ks[2], (DEPTH, D_MODEL, D_FF), D_MODEL ** -0.5),
        "ffn1_w_up": nrm(ks[3], (DEPTH, D_MODEL, D_FF), D_MODEL ** -0.5),
        "ffn1_w_down": nrm(ks[4], (DEPTH, D_FF, D_MODEL), D_FF ** -0.5),
        "mix_norm": gain(ks[5], D_MODEL),
        "w_in": nrm(ks[6], (DEPTH, D_MODEL, D_IN_PROJ), D_MODEL ** -0.5),
        "w_out": nrm(ks[7], (DEPTH, D_MIX, D_MODEL), D_MIX ** -0.5),
        "mlstm_conv_w": nrm(ks[8], (DEPTH, M_CONV, 2 * M_WIDTH), M_CONV ** -0.5),
        "mlstm_conv_b": nrm(ks[9], (DEPTH, 2 * M_WIDTH), 0.02),
        "mlstm_gate_b": gate_base + gate_noise,
        "mlstm_norm": gain(ks[11], M_WIDTH),
        "ssd_conv_w": nrm(ks[12], (DEPTH, S_CONV, S_WIDTH + 2 * S_GROUPS * S_STATE), S_CONV ** -0.5),
        "ssd_conv_b": nrm(ks[13], (DEPTH, S_WIDTH + 2 * S_GROUPS * S_STATE), 0.02),
        "ssd_dt_bias": dt_bias,
        "ssd_a_log": a_log,
        "ssd_d": 1.0 + nrm(ks[16], (DEPTH, S_HEADS), 0.1),
        "ssd_norm": gain(ks[17], S_WIDTH),
        "cm_conv_w": nrm(ks[18], (DEPTH, C_KERNEL, C_WIDTH), C_KERNEL ** -0.5),
        "cm_conv_b": nrm(ks[19], (DEPTH, C_WIDTH), 0.02),
        "cm_ln_g": gain(ks[20], C_WIDTH),
        "cm_ln_b": nrm(ks[21], (DEPTH, C_WIDTH), 0.02),
        "ffn2_norm": gain(ks[22], D_MODEL),
        "ffn2_w_gate": nrm(ks[23], (DEPTH, D_MODEL, D_FF), D_MODEL ** -0.5),
        "ffn2_w_up": nrm(ks[24], (DEPTH, D_MODEL, D_FF), D_MODEL ** -0.5),
        "ffn2_w_down": nrm(ks[25], (DEPTH, D_FF, D_MODEL), D_FF ** -0.5),
        "final_norm": 1.0 + nrm(ks[26], (D_MODEL,), 0.02),
    }


def reference(x, ffn1_norm, ffn1_w_gate, ffn1_w_up, ffn1_w_down, mix_norm, w_in, w_out,
              mlstm_conv_w, mlstm_conv_b, mlstm_gate_b, mlstm_norm,
              ssd_conv_w, ssd_conv_b, ssd_dt_bias, ssd_a_log, ssd_d, ssd_norm,
              cm_conv_w, cm_conv_b, cm_ln_g, cm_ln_b,
              ffn2_norm, ffn2_w_gate, ffn2_w_up, ffn2_w_down, final_norm):
    for l in range(DEPTH):
        x = x + 0.5 * swiglu(rmsnorm(x, ffn1_norm[l]), ffn1_w_gate[l], ffn1_w_up[l], ffn1_w_down[l])
        x = x + hybrid_mixer(rmsnorm(x, mix_norm[l]), w_in[l], w_out[l],
                             mlstm_conv_w[l], mlstm_conv_b[l], mlstm_gate_b[l], mlstm_norm[l],
                             ssd_conv_w[l], ssd_conv_b[l], ssd_dt_bias[l], ssd_a_log[l], ssd_d[l], ssd_norm[l],
                             cm_conv_w[l], cm_conv_b[l], cm_ln_g[l], cm_ln_b[l])
        x = x + 0.5 * swiglu(rmsnorm(x, ffn2_norm[l]), ffn2_w_gate[l], ffn2_w_up[l], ffn2_w_down[l])
    return rmsnorm(x, final_norm)
```

```python
import math
import os
import types
from contextlib import ExitStack
import numpy as np
import concourse.bass as bass
import concourse.mybir as mybir
from concourse.bass_utils import run_bass_kernel_spmd

F32 = mybir.dt.float32
BF16 = mybir.dt.bfloat16
AF = mybir.ActivationFunctionType
ALU = mybir.AluOpType

D = 1024
DFF = 2816
NF = 22
NCORES = 8
SEQ = 16384
NSEG = 4
NT_FULL = SEQ // NSEG
HALO = 32
TW = 512
RMS_EPS = 1e-6
LN_EPS = 1e-5
LNSCALE = math.log(128 ** -0.5)
NEG = -30000.0


class Res:
    __slots__ = ("name", "writer", "readers", "track")

    def __init__(self, name):
        self.name = name
        self.track = name
        self.writer = None
        self.readers = []


class Op:
    __slots__ = ("queue", "track", "fn", "deps", "needs_inc", "semval", "is_dma")


SYNC_SAME = set(os.environ.get('K_SYNC_SAME', 'pool,act,dve').split(','))


class Prog:
    QUEUES = ("pe", "dve", "act", "pool", "sp")

    def __init__(self, nc):
        self.nc = nc
        self.ops = {q: [] for q in self.QUEUES}
        self.tracks = {}

    @staticmethod
    def _freeze(fn):
        if fn.__closure__ is None:
            return fn
        cells = []
        for c in fn.__closure__:
            try:
                cells.append(types.CellType(c.cell_contents))
            except ValueError:
                cells.append(c)
        return types.FunctionType(fn.__code__, fn.__globals__, fn.__name__, fn.__defaults__, tuple(cells))

    def _add(self, queue, track, fn, reads, writes, is_dma):
        fn = self._freeze(fn)
        op = Op()
        op.queue, op.track, op.fn, op.is_dma = queue, track, fn, is_dma
        op.needs_inc = is_dma
        op.deps = []
        deps = {}
        for r in reads:
            if r.writer is not None:
                deps[id(r.writer)] = r.writer
        for w in writes:
            if w.writer is not None:
                deps[id(w.writer)] = w.writer
            for rd in w.readers:
                deps[id(rd)] = rd
        for d in deps.values():
            if (not d.is_dma) and d.queue == queue and not is_dma and queue not in SYNC_SAME:
                continue
            op.deps.append(d)
            d.needs_inc = True
        self.ops[queue].append(op)
        self.tracks.setdefault(track, []).append(op)
        for r in reads:
            r.readers.append(op)
        for w in writes:
            w.writer = op
            w.readers = []
        return op

    def op(self, queue, fn, reads=(), writes=()):
        return self._add(queue, queue, fn, reads, writes, False)

    def dma(self, queue, track, fn, reads=(), writes=()):
        return self._add(queue, "dma_" + track, fn, reads, writes, True)

    def emit(self):
        nc = self.nc
        for tname, tops in self.tracks.items():
            v = 0
            for o in tops:
                if o.needs_inc:
                    v += 16 if o.is_dma else 1
                    o.semval = v
                else:
                    o.semval = None
        finals = {}
        for t, tops in self.tracks.items():
            if t.startswith("dma_"):
                finals.setdefault(tops[-1].queue, []).append(tops[-1])
        with ExitStack() as st:
            sems = {t: st.enter_context(nc.semaphore("s_" + t)) for t in self.tracks}
            block = st.enter_context(nc.Block())
            handles = {"pe": block.tensor, "dve": block.vector, "act": block.scalar,
                       "pool": block.gpsimd, "sp": block.sync}

            def make(q):
                def body(eng):
                    waited = {}
                    for o in self.ops[q]:
                        need = {}
                        for d in o.deps:
                            if need.get(d.track, 0) < d.semval:
                                need[d.track] = d.semval
                        for t, v in need.items():
                            if waited.get(t, 0) < v:
                                eng.wait_ge(sems[t], v)
                                waited[t] = v
                        ins = o.fn(eng)
                        if o.needs_inc:
                            ins.then_inc(sems[o.track], 16 if o.is_dma else 1)
                    for o in finals.get(q, []):
                        eng.wait_ge(sems[o.track], o.semval)
                return body

            for q in self.QUEUES:
                handles[q](make(q))


class Rot:
    def __init__(self, nc, name, n, shape, dtype, psum=False):
        self.bufs = []
        for i in range(n):
            if psum:
                t = nc.alloc_psum_tensor(f"{name}{i}", shape, dtype)
            else:
                t = nc.alloc_sbuf_tensor(f"{name}{i}", shape, dtype)
            self.bufs.append((t, Res(f"{name}{i}")))
        self.i = 0

    def get(self):
        b = self.bufs[self.i % len(self.bufs)]
        self.i += 1
        return b


class SubRot:
    def __init__(self, nc, name, nbanks, per_bank, width, dtype):
        self.bufs = []
        tot = 512 if dtype == F32 else 1024
        for b in range(nbanks):
            t = nc.alloc_psum_tensor(f"{name}{b}", [128, tot], dtype)
            for j in range(per_bank):
                self.bufs.append((t, j * (tot // per_bank), Res(f"{name}{b}_{j}")))
        self.i = 0
        self.width = width

    def get(self, w=None):
        t, off, r = self.bufs[self.i % len(self.bufs)]
        self.i += 1
        w = self.width if w is None else w
        return t[:, off:off + w], r


PAN_Q, PAN_K, PAN_O, PAN_XS, PAN_B, PAN_C, PAN_G = 0, 4, 8, 12, 20, 24, 28
NPAN = 36


def build_layer(NT, final_norm):
    nc = bass.Bass("TRN2", target_bir_lowering=False)
    P = Prog(nc)
    ntiles = NT // TW

    def din(name, shape, dt=F32):
        return nc.dram_tensor(name, list(shape), dt, kind="ExternalInput").ap()

    def dout(name, shape, dt=F32):
        return nc.dram_tensor(name, list(shape), dt, kind="ExternalOutput").ap()

    def dscr(name, shape, dt=BF16):
        return nc.dram_tensor(name, list(shape), dt).ap()

    x_d = din("x", [HALO + NT, D])
    y_d = dout("y", [NT, D])
    w_f32 = {
        "gu1": din("gu1", [NF, 128, 2048]), "d1": din("d1", [8, 128, NF * 128]),
        "gu2": din("gu2", [NF, 128, 2048]), "d2": din("d2", [8, 128, NF * 128]),
        "winf": din("winf", [NPAN, 128, 1024]), "wtok": din("wtok", [7, 128, 2048]),
        "wout": din("wout", [8, 128, 2048]),
    }
    w_bf = {k: dscr(k + "_bf", v.shape) for k, v in w_f32.items()}
    w_res = {k: [Res(f"{k}_{i}") for i in range(v.shape[0])] for k, v in w_f32.items()}
    cols_d = din("cols", [128, 320])
    rows_d = din("rows", [1, 1088])
    st_m_in = din("st_m_in", [4, 128, 129]); st_s_in = din("st_s_in", [4, 128, 256])
    st_m_out = dout("st_m_out", [4, 128, 129]); st_s_out = dout("st_s_out", [4, 128, 256])

    sb = nc.alloc_sbuf_tensor

    ident_f = sb("ident_f", [128, 128], F32); ident_b = sb("ident_b", [128, 128], BF16)
    ones_f = sb("ones_f", [128, 128], F32); ones_b = sb("ones_b", [128, 128], BF16)
    tri_f = sb("tri_f", [128, 128], F32); gt_f = sb("gt_f", [128, 128], F32); negI = sb("negI", [128, 128], F32)
    r_const = Res("const")
    cols = sb("cols_sb", [128, 320], F32); rows = sb("rows_sb", [128, 1088], F32)
    r_cols = Res("cols"); r_rows = Res("rows")

    def cst(q, fn):
        P.op(q, fn, writes=[r_const])
    cst("pool", lambda e: e.memset(ones_f[:], 1.0))
    cst("pool", lambda e: e.memset(ones_b[:], 1.0))
    cst("pool", lambda e: e.memset(ident_f[:], 1.0))
    cst("pool", lambda e: e.affine_select(out=ident_f[:], in_=ident_f[:], pattern=[[-1, 128]], compare_op=ALU.is_equal,
                                          fill=0.0, base=0, channel_multiplier=1))
    cst("pool", lambda e: e.tensor_copy(out=ident_b[:], in_=ident_f[:]))
    cst("pool", lambda e: e.memset(tri_f[:], 1.0))
    cst("pool", lambda e: e.affine_select(out=tri_f[:], in_=tri_f[:], pattern=[[1, 128]], compare_op=ALU.is_ge,
                                          fill=0.0, base=0, channel_multiplier=-1))
    cst("pool", lambda e: e.memset(gt_f[:], 1.0))
    cst("pool", lambda e: e.affine_select(out=gt_f[:], in_=gt_f[:], pattern=[[-1, 128]], compare_op=ALU.is_gt,
                                          fill=0.0, base=0, channel_multiplier=1))
    cst("pool", lambda e: e.tensor_scalar(out=negI[:], in0=ident_f[:], scalar1=NEG, scalar2=None, op0=ALU.mult))
    P.dma("sp", "cols", lambda e: e.dma_start(out=cols[:], in_=cols_d), writes=[r_cols])
    P.dma("sp", "rows", lambda e: e.dma_start(out=rows[:], in_=rows_d.partition_broadcast(128)), writes=[r_rows])

    C_G1, C_GM, C_G2, C_GF = 0, 8, 16, 24
    C_CW, C_CB = 32, 128
    C_MN = 152
    C_C31W, C_C31B, C_LNG, C_LNB = 156, 280, 284, 288
    R_GB, R_DTB, R_ALOG, R_D, R_SN = 0, 8, 24, 40, 56
    cols_big = None

    tok = sb("casttok", [128, 8], F32)
    for k in ("gu1", "d1", "winf", "wtok", "wout", "gu2", "d2"):
        if os.environ.get("K_NOCAST"):
            break
        r_k = Res("cast_" + k)
        for i in range(w_f32[k].shape[0]):
            P.dma("pool", "cast_" + k, (lambda kk, ii: lambda e: e.dma_start(out=w_bf[kk][ii], in_=w_f32[kk][ii]))(k, i), writes=[r_k])
        P.op("pool", lambda e: e.memset(tok[:], 0.0), reads=[r_k], writes=w_res[k])

    CT = [sb(f"CT{h}", [128, 129], F32) for h in range(4)]; r_CT = [Res(f"CT{h}") for h in range(4)]
    CTb = [sb(f"CTb{h}", [128, 128], BF16) for h in range(4)]; r_CTb = [Res(f"CTb{h}") for h in range(4)]
    nB = [sb(f"nB{h}", [128, 128], BF16) for h in range(4)]; r_nB = [Res(f"nB{h}") for h in range(4)]
    HT = [sb(f"HT{g}", [128, 256], F32) for g in range(4)]; r_HT = [Res(f"HT{g}") for g in range(4)]
    HTb = [sb(f"HTb{g}", [128, 256], BF16) for g in range(4)]; r_HTb = [Res(f"HTb{g}") for g in range(4)]
    for h in range(4):
        P.dma("sp", f"ctin{h}", (lambda hh: lambda e: e.dma_start(out=CT[hh][:], in_=st_m_in[hh]))(h), writes=[r_CT[h]])
        P.dma("sp", f"htin{h}", (lambda hh: lambda e: e.dma_start(out=HT[hh][:], in_=st_s_in[hh]))(h), writes=[r_HT[h]])

    def refresh_m(h):
        P.op("act", lambda e: e.activation(out=CTb[h][:], in_=CT[h][:, 0:128], func=AF.Copy), reads=[r_CT[h]], writes=[r_CTb[h]])
        P.op("act", lambda e: e.activation(out=nB[h][:], in_=ones_f[:], func=AF.Copy, scale=CT[h][:, 128:129]),
             reads=[r_CT[h], r_const], writes=[r_nB[h]])

    def refresh_s(g):
        P.op("act", lambda e: e.activation(out=HTb[g][:], in_=HT[g][:], func=AF.Copy), reads=[r_HT[g]], writes=[r_HTb[g]])
    for h in range(4):
        refresh_m(h); refresh_s(h)

    xin = Rot(nc, "xin", 2, [128, D], F32)
    xT = sb("xT", [128, 8, TW], F32); r_xT = [Res(f"xT{k}") for k in range(8)]
    mix = sb("mix", [128, 16, TW], BF16)
    hbf = mix[:, 0:8, :]; r_hbf = [Res(f"hbf{k}") for k in range(8)]
    arena = sb("arena", [128, 24 * (TW + 3)], BF16)
    hid = arena[:, 0:NF * TW].rearrange("p (f t) -> p f t", f=NF); r_hid = [Res(f"hid{f}") for f in range(NF)]
    convin = arena[:, :].rearrange("p (j t) -> p j t", j=24); r_cin = [Res(f"cin{j}") for j in range(24)]
    r_A1 = Res("arenaA1"); r_A2 = Res("arenaA2")
    halo4 = sb("halo4", [128, 24, 3], BF16); r_halo4 = Res("halo4")
    cvo = sb("cvo", [128, 24, TW], BF16); r_cvo = [Res(f"cvo{j}") for j in range(24)]
    for j in range(4):
        r_cvo[2 * j + 1] = r_cvo[2 * j]
    sigo = sb("sigo", [128, 4, TW], BF16); r_sigo = [Res(f"sigo{j}") for j in range(4)]
    ubuf = sb("ubuf", [128, 4, 30 + TW], F32); r_ub = [Res(f"ub{j}") for j in range(4)]
    cacc = cvo[:, 0:8, :].rearrange("p j t -> p (j t)").bitcast(F32).rearrange("p (j t) -> p j t", j=4)
    r_cacc = [r_cvo[2 * j] for j in range(4)]
    vtok = sb("vtok", [128, 4, 512], BF16); r_vtok = [Res(f"vtok{c}") for c in range(4)]
    zs = sb("zs", [128, 4, 1024], BF16); r_zs = [Res(f"zs{c}") for c in range(4)]
    gates = sb("gates", [128, 4, 24], F32); r_gates = [Res(f"gates{c}") for c in range(4)]
    r_mix = [[(r_hbf[k] if k < 8 else Res(f"mix{k}_{c}")) for c in range(4)] for k in range(16)]
    gupan = Rot(nc, "gupan", 2, [128, 2, 8, 128], BF16)
    dpan = Rot(nc, "dpan", 1, [128, NF, 128], BF16)
    wpan = Rot(nc, "wpan", 2, [128, 8, 128], BF16)
    tpan = Rot(nc, "tpan", 1, [128, 8, 256], BF16)
    opan = Rot(nc, "opan", 2, [128, 16, 128], BF16)
    f512 = Rot(nc, "f512", 3, [128, TW], F32)
    f128 = Rot(nc, "f128", 7, [128, 128], F32)
    b128 = Rot(nc, "b128", 8, [128, 128], BF16)
    b129 = Rot(nc, "b129", 2, [128, 129], BF16)
    f256 = Rot(nc, "f256", 4, [128, 256], F32)
    j256 = Rot(nc, "j256", 1, [128, 256], BF16)
    b256 = Rot(nc, "b256", 4, [128, 256], BF16)
    xdt = Rot(nc, "xdt", 1, [128, 1024], BF16)
    xsd = Rot(nc, "xsd", 1, [128, 1024], BF16)
    btok = Rot(nc, "btok", 1, [128, 512], BF16)
    sm = Rot(nc, "sm", 22, [128, 24], F32)
    rstd_t = sb("rstd_t", [128, TW], F32); r_rstd = Res("rstd")
    big = Rot(nc, "pbig", 3, [128, 512], F32, psum=True)
    sml = SubRot(nc, "psml", 3, 1, 128, F32)
    med = sml
    ptr = Rot(nc, "ptr", 2, [128, 1024], BF16, psum=True)
    kt4p = Rot(nc, "kt4", 2, [128, 512], BF16)
    lt4 = Rot(nc, "lt4", 5, [128, 128], BF16)
    xd256 = Rot(nc, "xd256", 2, [128, 256], BF16)

    def load_x(row0, n, dst_c0):
        xt, rx = xin.get()
        P.dma("pool", rx.track, lambda e: e.dma_start(out=xt[0:n, :], in_=x_d[row0:row0 + n, :]), writes=[rx])
        for half in range(2):
            pt, rp = big.get()
            for j in range(4):
                kc = half * 4 + j
                P.op("pe", (lambda kc, j: lambda e: e.transpose(out=pt[:, j * 128:j * 128 + n], in_=xt[0:n, kc * 128:(kc + 1) * 128],
                                                                  identity=ident_f[0:n, 0:n]))(kc, j),
                     reads=[rx, r_const], writes=[rp])
            P.op("act", lambda e: e.activation(out=xT[:, half * 4:half * 4 + 4, dst_c0:dst_c0 + n],
                                                in_=pt[:, :].rearrange("p (j t) -> p j t", j=4)[:, :, 0:n], func=AF.Copy),
                 reads=[rp], writes=r_xT[half * 4:half * 4 + 4])

    def rms_to_hbf(TN, gcol):
        ps, rp = big.get()
        for kc in range(8):
            sq, rs = f512.get()
            P.op("act", (lambda kc: lambda e: e.activation(out=sq[:, 0:TN], in_=xT[:, kc, 0:TN], func=AF.Square))(kc),
                 reads=[r_xT[kc]], writes=[rs])
            P.op("pe", (lambda kc: lambda e: e.matmul(ps[:, 0:TN], lhsT=ones_f[:], rhs=sq[:, 0:TN], start=(kc == 0), stop=(kc == 7)))(kc),
                 reads=[rs, r_const], writes=[rp])
        P.op("act", lambda e: e.activation(out=rstd_t[:, 0:TN], in_=ps[:, 0:TN], func=AF.Sqrt, scale=1.0 / D, bias=RMS_EPS),
             reads=[rp], writes=[r_rstd])
        P.op("dve", lambda e: e.reciprocal(out=rstd_t[:, 0:TN], in_=rstd_t[:, 0:TN]), reads=[r_rstd], writes=[r_rstd])
        for kc in range(8):
            P.op("dve", (lambda kc: lambda e: e.scalar_tensor_tensor(out=hbf[:, kc, 0:TN], in0=xT[:, kc, 0:TN],
                                                                      scalar=cols[:, gcol + kc:gcol + kc + 1], in1=rstd_t[:, 0:TN],
                                                                      op0=ALU.mult, op1=ALU.mult))(kc),
                 reads=[r_xT[kc], r_rstd, r_cols], writes=[r_hbf[kc]])

    def ffn(TN, gu_key, d_key, gcol):
        rms_to_hbf(TN, gcol)
        for f in range(NF):
            gp, rg = gupan.get()
            P.dma("sp", rg.track, (lambda f: lambda e: e.dma_start(out=gp[:].rearrange("p a k c -> p (a k c)"), in_=w_bf[gu_key][f]))(f),
                  reads=[w_res[gu_key][f]], writes=[rg])
            pg, rpg = big.get()
            pu, rpu = big.get()
            for a, (pp, rpp) in enumerate(((pg, rpg), (pu, rpu))):
                for kc in range(8):
                    P.op("pe", (lambda a, kc, pp: lambda e: e.matmul(pp[:, 0:TN], lhsT=gp[:, a, kc, :], rhs=hbf[:, kc, 0:TN],
                                                                      start=(kc == 0), stop=(kc == 7)))(a, kc, pp),
                         reads=[rg, r_hbf[kc]], writes=[rpp])
            sg, rsg = f512.get()
            P.op("act", lambda e: e.activation(out=sg[:, 0:TN], in_=pg[:, 0:TN], func=AF.Silu), reads=[rpg], writes=[rsg])
            P.op("dve", (lambda f: lambda e: e.tensor_tensor(out=hid[:, f, 0:TN], in0=sg[:, 0:TN], in1=pu[:, 0:TN], op=ALU.mult))(f),
                 reads=[rsg, rpu, r_A1] if f else [rsg, rpu], writes=[r_hid[f]] if f else [r_hid[f], r_A1])
        for m in range(8):
            dp, rd = dpan.get()
            P.dma("sp", rd.track, (lambda m: lambda e: e.dma_start(out=dp[:].rearrange("p f c -> p (f c)"), in_=w_bf[d_key][m]))(m),
                  reads=[w_res[d_key][m]], writes=[rd])
            pd, rpd = big.get()
            for f in range(NF):
                P.op("pe", (lambda f: lambda e: e.matmul(pd[:, 0:TN], lhsT=dp[:, f, :], rhs=hid[:, f, 0:TN],
                                                          start=(f == 0), stop=(f == NF - 1)))(f),
                     reads=[rd, r_hid[f], r_A2], writes=[rpd])
            P.op("dve", (lambda m: lambda e: e.scalar_tensor_tensor(out=xT[:, m, 0:TN], in0=pd[:, 0:TN], scalar=0.5, in1=xT[:, m, 0:TN],
                                                                     op0=ALU.mult, op1=ALU.add))(m),
                 reads=[rpd, r_xT[m]], writes=[r_xT[m]])

    def inproj_feature(TN, halo):
        pga = None
        for j in range(NPAN):
            if halo and PAN_O <= j < PAN_XS:
                continue
            wp, rw = wpan.get()
            P.dma("sp", rw.track, (lambda j: lambda e: e.dma_start(out=wp[:].rearrange("p k c -> p (k c)"), in_=w_bf["winf"][j]))(j),
                  reads=[w_res["winf"][j]], writes=[rw])
            pp, rp = big.get()
            for kc in range(8):
                P.op("pe", (lambda kc: lambda e: e.matmul(pp[:, 0:TN], lhsT=wp[:, kc, :], rhs=hbf[:, kc, 0:TN],
                                                           start=(kc == 0), stop=(kc == 7)))(kc),
                     reads=[rw, r_hbf[kc]], writes=[rp])
            if j < PAN_O or PAN_XS <= j < PAN_G:
                cj = j if j < PAN_O else 8 + (j - PAN_XS)
                if halo:
                    P.op("act", (lambda cj: lambda e: e.activation(out=halo4[:, cj, :], in_=pp[:, TN - 3:TN], func=AF.Copy))(cj),
                         reads=[rp], writes=[r_halo4])
                else:
                    P.op("act", (lambda cj: lambda e: e.activation(out=convin[:, cj, 3:3 + TN], in_=pp[:, 0:TN], func=AF.Copy))(cj),
                         reads=[rp, r_A1], writes=[r_cin[cj], r_A2])
            elif j < PAN_XS:
                oj = j - PAN_O
                P.op("act", (lambda oj: lambda e: e.activation(out=sigo[:, oj, 0:TN], in_=pp[:, 0:TN], func=AF.Sigmoid))(oj),
                     reads=[rp], writes=[r_sigo[oj]])
            else:
                jj, isb = (j - PAN_G) // 2, (j - PAN_G) % 2
                if not isb:
                    pga = (pp, rp)
                else:
                    sg, rsg = f512.get()
                    P.op("act", lambda e: e.activation(out=sg[:, 0:TN], in_=pp[:, 0:TN], func=AF.Sigmoid), reads=[rp], writes=[rsg])
                    pa, rpa = pga
                    if halo:
                        P.op("dve", (lambda jj, pa: lambda e: e.tensor_tensor(out=ubuf[:, jj, 0:30], in0=pa[:, TN - 30:TN], in1=sg[:, TN - 30:TN], op=ALU.mult))(jj, pa),
                             reads=[rpa, rsg], writes=[r_ub[jj]])
                    else:
                        P.op("dve", (lambda jj, pa: lambda e: e.tensor_tensor(out=ubuf[:, jj, 30:30 + TN], in0=pa[:, 0:TN], in1=sg[:, 0:TN], op=ALU.mult))(jj, pa),
                             reads=[rpa, rsg], writes=[r_ub[jj]])

    def inproj_token():
        for pi in range(7):
            tp, rt = tpan.get()
            ncol = 256 if pi < 6 else 24
            P.dma("sp", rt.track, (lambda pi: lambda e: e.dma_start(out=tp[:].rearrange("p k c -> p (k c)"), in_=w_bf["wtok"][pi]))(pi),
                  reads=[w_res["wtok"][pi]], writes=[rt])
            for c in range(4):
                pp, rp = med.get(256)
                for kc in range(8):
                    P.op("pe", (lambda kc, c: lambda e: e.matmul(pp[:, 0:ncol], lhsT=hbf[:, kc, c * 128:(c + 1) * 128], rhs=tp[:, kc, 0:ncol],
                                                                  start=(kc == 0), stop=(kc == 7)))(kc, c),
                         reads=[rt, r_hbf[kc]], writes=[rp])
                if pi < 2:
                    P.op("act", (lambda c, pi: lambda e: e.activation(out=vtok[:, c, pi * 256:(pi + 1) * 256], in_=pp, func=AF.Copy))(c, pi),
                         reads=[rp], writes=[r_vtok[c]])
                elif pi < 6:
                    P.op("act", (lambda c, pi: lambda e: e.activation(out=zs[:, c, (pi - 2) * 256:(pi - 1) * 256], in_=pp, func=AF.Silu))(c, pi),
                         reads=[rp], writes=[r_zs[c]])
                else:
                    P.op("dve", (lambda c: lambda e: e.tensor_copy(out=gates[:, c, :], in_=pp[:, 0:24]))(c), reads=[rp], writes=[r_gates[c]])

    def conv4():
        P.op("pool", lambda e: e.tensor_copy(out=convin[:, :, 0:3], in_=halo4[:, :, :]), reads=[r_halo4, r_A1], writes=r_cin + [r_A2])
        conv4_body()
        P.op("pool", lambda e: e.tensor_copy(out=halo4[:, :, :], in_=convin[:, :, TW:TW + 3]), reads=r_cin + [r_A1], writes=[r_halo4])

    def conv4_body():
        for j in range(24):
            acc, ra = f512.get()
            cw = lambda t, j=j: cols[:, C_CW + j * 4 + t:C_CW + j * 4 + t + 1]
            P.op("dve", (lambda j, cw: lambda e: e.tensor_scalar(out=acc[:, :], in0=convin[:, j, 0:TW], scalar1=cw(0), scalar2=None, op0=ALU.mult))(j, cw),
                 reads=[r_cin[j], r_cols, r_A1], writes=[ra])
            for t in range(1, 4):
                P.op("dve", (lambda j, t, cw: lambda e: e.scalar_tensor_tensor(out=acc[:, :], in0=convin[:, j, t:t + TW], scalar=cw(t), in1=acc[:, :],
                                                                                op0=ALU.mult, op1=ALU.add))(j, t, cw),
                     reads=[r_cin[j], ra, r_A1], writes=[ra])
            P.op("act", (lambda j: lambda e: e.activation(out=cvo[:, j, :], in_=acc[:, :], func=AF.Silu, bias=cols[:, C_CB + j:C_CB + j + 1]))(j),
                 reads=[ra, r_cols], writes=[r_cvo[j]])

    def gate_prep(c):
        g = {}
        pre, rpre = sm.get()
        P.op("dve", lambda e: e.tensor_tensor(out=pre[:, 0:24], in0=gates[:, c, :], in1=rows[:, R_GB:R_GB + 24], op=ALU.add),
             reads=[r_gates[c], r_rows], writes=[rpre])
        ex, rex = sm.get()
        P.op("act", lambda e: e.activation(out=ex[:, 4:8], in_=pre[:, 4:8], func=AF.Exp, scale=-1.0), reads=[rpre], writes=[rex])
        P.op("act", lambda e: e.activation(out=ex[:, 8:24], in_=pre[:, 8:24], func=AF.Exp), reads=[rpre], writes=[rex])
        P.op("act", lambda e: e.activation(out=ex[:, 4:24], in_=ex[:, 4:24], func=AF.Ln, bias=1.0), reads=[rex], writes=[rex])
        L, rL = sm.get()
        P.op("dve", lambda e: e.tensor_scalar(out=L[:, 0:4], in0=ex[:, 4:8], scalar1=-1.0, scalar2=None, op0=ALU.mult), reads=[rex], writes=[rL])
        P.op("dve", lambda e: e.tensor_tensor(out=L[:, 4:20], in0=ex[:, 8:24], in1=negA[:, 0:16], op=ALU.mult), reads=[rex, r_negA], writes=[rL])
        pc, rpc = sml.get()
        P.op("pe", lambda e: e.matmul(pc[:, 0:20], lhsT=tri_f[:], rhs=L[:, 0:20], start=True, stop=True), reads=[rL, r_const], writes=[rpc])
        P.op("pe", lambda e: e.matmul(pc[:, 32:52], lhsT=ones_f[:], rhs=L[:, 0:20], start=True, stop=True), reads=[rL, r_const], writes=[rpc])
        cs, rcs = sm.get()
        P.op("dve", lambda e: e.tensor_copy(out=cs[:, 0:20], in_=pc[:, 0:20]), reads=[rpc], writes=[rcs])
        tot, rtot = sm.get()
        P.op("dve", lambda e: e.tensor_copy(out=tot[:, 0:20], in_=pc[:, 32:52]), reads=[rpc], writes=[rtot])
        wb, rwb = sm.get()
        P.op("dve", lambda e: e.scalar_tensor_tensor(out=wb[:, 0:4], in0=pre[:, 0:4], scalar=LNSCALE, in1=cs[:, 0:4], op0=ALU.add, op1=ALU.subtract),
             reads=[rpre, rcs], writes=[rwb])
        P.op("dve", lambda e: e.tensor_scalar(out=wb[:, 4:20], in0=cs[:, 4:20], scalar1=-1.0, scalar2=None, op0=ALU.mult), reads=[rcs], writes=[rwb])
        ee, ree = sm.get()
        P.op("dve", lambda e: e.tensor_tensor(out=ee[:, 0:20], in0=wb[:, 0:20], in1=tot[:, 0:20], op=ALU.add), reads=[rwb, rtot], writes=[ree])
        P.op("act", lambda e: e.activation(out=ee[:, 0:20], in_=ee[:, 0:20], func=AF.Exp), reads=[ree], writes=[ree])
        et, ret = sm.get()
        P.op("act", lambda e: e.activation(out=et[:, 0:20], in_=tot[:, 0:20], func=AF.Exp), reads=[rtot], writes=[ret])
        ec, rec = sm.get()
        P.op("act", lambda e: e.activation(out=ec[:, 0:20], in_=cs[:, 0:20], func=AF.Exp), reads=[rcs], writes=[rec])
        g.update(L=(L, rL), dt=(ex, rex), wb=(wb, rwb), ee=(ee, ree), et=(et, ret), ec=(ec, rec))
        return g

    def mlstm_chunk(c, g):
        cs_ = slice(c * 128, (c + 1) * 128)
        L, rL = g["L"]; wb, rwb = g["wb"]; ee, ree = g["ee"]; et, ret = g["et"]
        ptk, rptk = ptr.get()
        for h in range(4):
            P.op("pe", (lambda h: lambda e: e.transpose(out=ptk[:, h * 128:(h + 1) * 128], in_=cvo[:, 4 + h, cs_], identity=ident_b[:]))(h),
                 reads=[r_cvo[4 + h], r_const], writes=[rptk])
        kt4, rkt4 = kt4p.get()
        P.op("act", lambda e: e.activation(out=kt4[:], in_=ptk[:, 0:512], func=AF.Copy), reads=[rptk], writes=[rkt4])
        for h in range(4):
            TL, rTL = f128.get()
            P.op("dve", (lambda h: lambda e: e.tensor_scalar(out=TL[:], in0=tri_f[:], scalar1=L[:, h:h + 1], scalar2=None, op0=ALU.mult))(h),
                 reads=[rL, r_const], writes=[rTL])
            pb1, rpb1 = sml.get()
            P.op("pe", lambda e: e.matmul(pb1, lhsT=ones_f[:], rhs=TL[:], start=True, stop=True), reads=[rTL, r_const], writes=[rpb1])
            Arow, rAr = f128.get()
            P.op("act", lambda e: e.activation(out=Arow[:], in_=pb1, func=AF.Exp), reads=[rpb1], writes=[rAr])
            pb2, rpb2 = sml.get()
            P.op("pe", lambda e: e.matmul(pb2, lhsT=ones_f[:], rhs=TL[:], start=True, stop=False), reads=[rTL, r_const], writes=[rpb2])
            P.op("pe", lambda e: e.matmul(pb2, lhsT=negI[:], rhs=gt_f[:], start=False, stop=True), reads=[r_const], writes=[rpb2])
            wT, rwT = f128.get()
            P.op("act", (lambda h: lambda e: e.activation(out=wT[:], in_=pb2, func=AF.Exp, bias=wb[:, h:h + 1]))(h), reads=[rpb2, rwb], writes=[rwT])
            pqk, rpqk = sml.get()
            P.op("pe", (lambda h: lambda e: e.matmul(pqk, lhsT=cvo[:, 4 + h, cs_], rhs=cvo[:, h, cs_], start=True, stop=True))(h),
                 reads=[r_cvo[4 + h], r_cvo[h]], writes=[rpqk])
            sqk, rsqk = b128.get()
            P.op("dve", lambda e: e.tensor_tensor(out=sqk[:], in0=pqk, in1=wT[:], op=ALU.mult), reads=[rpqk, rwT], writes=[rsqk])
            qa, rqa = b128.get()
            P.op("dve", (lambda h: lambda e: e.tensor_tensor(out=qa[:], in0=cvo[:, h, cs_], in1=Arow[:], op=ALU.mult))(h),
                 reads=[r_cvo[h], rAr], writes=[rqa])
            pnum, rpn = sml.get()
            P.op("pe", (lambda h: lambda e: e.matmul(pnum, lhsT=vtok[:, c, h * 128:(h + 1) * 128], rhs=sqk[:], start=True, stop=False))(h),
                 reads=[r_vtok[c], rsqk], writes=[rpn])
            P.op("pe", (lambda h: lambda e: e.matmul(pnum, lhsT=CTb[h][:], rhs=qa[:], start=False, stop=True))(h), reads=[r_CTb[h], rqa], writes=[rpn])
            pden, rpd = sml.get()
            P.op("pe", lambda e: e.matmul(pden, lhsT=ones_b[:], rhs=sqk[:], start=True, stop=False), reads=[r_const, rsqk], writes=[rpd])
            P.op("pe", (lambda h: lambda e: e.matmul(pden, lhsT=nB[h][:], rhs=qa[:], start=False, stop=True))(h), reads=[r_nB[h], rqa], writes=[rpd])
            dn, rdn = f128.get()
            P.op("act", lambda e: e.activation(out=dn[:], in_=pden, func=AF.Abs), reads=[rpd], writes=[rdn])
            P.op("dve", lambda e: e.tensor_scalar(out=dn[:], in0=dn[:], scalar1=1.0, scalar2=None, op0=ALU.max), reads=[rdn], writes=[rdn])
            P.op("dve", lambda e: e.reciprocal(out=dn[:], in_=dn[:]), reads=[rdn], writes=[rdn])
            hg, rhg = f128.get()
            P.op("dve", lambda e: e.tensor_tensor(out=hg[:], in0=pnum, in1=dn[:], op=ALU.mult), reads=[rpn, rdn], writes=[rhg])
            P.op("dve", (lambda h: lambda e: e.tensor_tensor(out=hg[:], in0=hg[:], in1=sigo[:, h, cs_], op=ALU.mult))(h), reads=[rhg, r_sigo[h]], writes=[rhg])
            sq, rsq = f128.get()
            P.op("act", lambda e: e.activation(out=sq[:], in_=hg[:], func=AF.Square), reads=[rhg], writes=[rsq])
            pss, rpss = sml.get()
            P.op("pe", lambda e: e.matmul(pss, lhsT=ones_f[:], rhs=sq[:], start=True, stop=True), reads=[rsq, r_const], writes=[rpss])
            rs, rrs = f128.get()
            P.op("act", lambda e: e.activation(out=rs[:], in_=pss, func=AF.Sqrt, scale=1.0 / 128, bias=RMS_EPS), reads=[rpss], writes=[rrs])
            P.op("dve", lambda e: e.reciprocal(out=rs[:], in_=rs[:]), reads=[rrs], writes=[rrs])
            P.op("dve", (lambda h: lambda e: e.scalar_tensor_tensor(out=mix[:, h, cs_], in0=hg[:], scalar=cols[:, C_MN + h:C_MN + h + 1], in1=rs[:],
                                                                     op0=ALU.mult, op1=ALU.mult))(h),
                 reads=[rhg, rrs, r_cols], writes=[r_mix[h][c]])
            vw, rvw = b129.get()
            P.op("dve", (lambda h: lambda e: e.tensor_scalar(out=vw[:, 0:128], in0=vtok[:, c, h * 128:(h + 1) * 128], scalar1=ee[:, h:h + 1], scalar2=None, op0=ALU.mult))(h),
                 reads=[r_vtok[c], ree], writes=[rvw])
            P.op("dve", (lambda h: lambda e: e.tensor_copy(out=vw[:, 128:129], in_=ee[:, h:h + 1]))(h), reads=[ree], writes=[rvw])
            pst, rpst = med.get(256)
            P.op("pe", (lambda h: lambda e: e.matmul(pst[:, 0:129], lhsT=kt4[:, h * 128:(h + 1) * 128], rhs=vw[:], start=True, stop=True))(h), reads=[rkt4, rvw], writes=[rpst])
            P.op("dve", (lambda h: lambda e: e.scalar_tensor_tensor(out=CT[h][:], in0=CT[h][:], scalar=et[:, h:h + 1], in1=pst[:, 0:129], op0=ALU.mult, op1=ALU.add))(h),
                 reads=[r_CT[h], ret, rpst], writes=[r_CT[h]])
            refresh_m(h)

    def ssd_chunk(c, g):
        cs_ = slice(c * 128, (c + 1) * 128)
        L, rL = g["L"]; dt, rdt = g["dt"]; wb, rwb = g["wb"]; ee, ree = g["ee"]; et, ret = g["et"]; ec, rec = g["ec"]
        Xdt, rX = xdt.get(); xsD, rxd = xsd.get(); Bt, rBt = btok.get()
        ptx, rptx = ptr.get()
        for j in range(8):
            P.op("pe", (lambda j: lambda e: e.transpose(out=ptx[:, j * 128:(j + 1) * 128], in_=cvo[:, 8 + j, cs_], identity=ident_b[:]))(j),
                 reads=[r_cvo[8 + j], r_const], writes=[rptx])
        P.op("dve", lambda e: e.tensor_tensor(out=Xdt[:].rearrange("p (r q) -> p r q", r=16), in0=ptx[:, :].rearrange("p (r q) -> p r q", r=16),
                                              in1=dt[:, 8:24].unsqueeze(2).to_broadcast([128, 16, 64]), op=ALU.mult),
             reads=[rptx, rdt], writes=[rX])
        P.op("dve", lambda e: e.tensor_tensor(out=xsD[:].rearrange("p (r q) -> p r q", r=16), in0=ptx[:, :].rearrange("p (r q) -> p r q", r=16),
                                              in1=rows[:, R_D:R_D + 16].unsqueeze(2).to_broadcast([128, 16, 64]), op=ALU.mult),
             reads=[rptx, r_rows], writes=[rxd])
        ptb, rptb = ptr.get()
        for gi in range(4):
            P.op("pe", (lambda gi: lambda e: e.transpose(out=ptb[:, gi * 128:(gi + 1) * 128], in_=cvo[:, 16 + gi, cs_], identity=ident_b[:]))(gi),
                 reads=[r_cvo[16 + gi], r_const], writes=[rptb])
        P.op("act", lambda e: e.activation(out=Bt[:], in_=ptb[:, 0:512], func=AF.Copy), reads=[rptb], writes=[rBt])
        py = [big.get(), big.get()]
        for half in range(2):
            P.op("pe", (lambda half: lambda e: e.matmul(py[half][0][:, :], lhsT=ident_b[:], rhs=xsD[:, half * 512:(half + 1) * 512], start=True, stop=False))(half),
                 reads=[rxd, r_const], writes=[py[half][1]])
        for gi in range(4):
            LTs = []
            for r in range(4):
                hh = gi * 4 + r
                TA, rTA = f128.get()
                P.op("dve", (lambda hh: lambda e: e.tensor_scalar(out=TA[:], in0=tri_f[:], scalar1=L[:, 4 + hh:5 + hh], scalar2=None, op0=ALU.mult))(hh),
                     reads=[rL, r_const], writes=[rTA])
                pl, rpl = sml.get()
                P.op("pe", lambda e: e.matmul(pl, lhsT=ones_f[:], rhs=TA[:], start=True, stop=False), reads=[rTA, r_const], writes=[rpl])
                P.op("pe", lambda e: e.matmul(pl, lhsT=negI[:], rhs=gt_f[:], start=False, stop=True), reads=[r_const], writes=[rpl])
                LT, rLT = lt4.get()
                P.op("act", (lambda hh: lambda e: e.activation(out=LT[:], in_=pl, func=AF.Exp, bias=wb[:, 4 + hh:5 + hh]))(hh), reads=[rpl, rwb], writes=[rLT])
                LTs.append((LT, rLT))
            pcb, rpcb = sml.get()
            P.op("pe", (lambda gi: lambda e: e.matmul(pcb, lhsT=cvo[:, 16 + gi, cs_], rhs=cvo[:, 20 + gi, cs_], start=True, stop=True))(gi),
                 reads=[r_cvo[16 + gi], r_cvo[20 + gi]], writes=[rpcb])
            for r in range(4):
                hh = gi * 4 + r
                LT, rLT = LTs[r]
                MT, rMT = b128.get()
                P.op("dve", lambda e: e.tensor_tensor(out=MT[:], in0=pcb, in1=LT[:], op=ALU.mult), reads=[rpcb, rLT], writes=[rMT])
                half, hq = hh // 8, hh % 8
                P.op("pe", (lambda hh, half, hq: lambda e: e.matmul(py[half][0][:, hq * 64:(hq + 1) * 64], lhsT=MT[:], rhs=Xdt[:, hh * 64:(hh + 1) * 64],
                                                                     start=False, stop=(hq == 7)))(hh, half, hq),
                     reads=[rMT, rX], writes=[py[half][1]])
        rs4, rrs4 = sm.get()
        yg_l = []
        for gi in range(4):
            poff, rpo = med.get(256)
            P.op("pe", (lambda gi: lambda e: e.matmul(poff, lhsT=cvo[:, 20 + gi, cs_], rhs=HTb[gi][:], start=True, stop=True))(gi),
                 reads=[r_cvo[20 + gi], r_HTb[gi]], writes=[rpo])
            yo, ryo = f256.get()
            P.op("dve", (lambda gi: lambda e: e.tensor_tensor(out=yo[:].rearrange("p (r q) -> p r q", r=4), in0=poff.rearrange("p (r q) -> p r q", r=4),
                                                               in1=ec[:, 4 + 4 * gi:8 + 4 * gi].unsqueeze(2).to_broadcast([128, 4, 64]), op=ALU.mult))(gi),
                 reads=[rpo, rec], writes=[ryo])
            half, off = gi // 2, (gi % 2) * 256
            P.op("dve", (lambda half, off: lambda e: e.tensor_tensor(out=yo[:], in0=yo[:], in1=py[half][0][:, off:off + 256], op=ALU.add))(half, off),
                 reads=[ryo, py[half][1]], writes=[ryo])
            P.op("dve", (lambda gi: lambda e: e.tensor_tensor(out=yo[:], in0=yo[:], in1=zs[:, c, gi * 256:(gi + 1) * 256], op=ALU.mult))(gi),
                 reads=[ryo, r_zs[c]], writes=[ryo])
            junk, rj = j256.get()
            P.op("act", (lambda gi: lambda e: e.activation(out=junk[:], in_=yo[:], func=AF.Square, accum_out=rs4[:, gi:gi + 1]))(gi),
                 reads=[ryo], writes=[rj, rrs4])
            yg_l.append((yo, ryo))
        P.op("act", lambda e: e.activation(out=rs4[:, 0:4], in_=rs4[:, 0:4], func=AF.Sqrt, scale=1.0 / 256, bias=RMS_EPS), reads=[rrs4], writes=[rrs4])
        P.op("dve", lambda e: e.reciprocal(out=rs4[:, 0:4], in_=rs4[:, 0:4]), reads=[rrs4], writes=[rrs4])
        pty, rpty = ptr.get()
        for gi in range(4):
            yo, ryo = yg_l[gi]
            yn, ryn = b256.get()
            P.op("dve", (lambda gi, yo: lambda e: e.scalar_tensor_tensor(out=yn[:], in0=yo[:], scalar=rs4[:, gi:gi + 1], in1=rows[:, R_SN + gi * 256:R_SN + (gi + 1) * 256],
                                                                          op0=ALU.mult, op1=ALU.mult))(gi, yo),
                 reads=[ryo, rrs4, r_rows], writes=[ryn])
            for half in range(2):
                k8 = 2 * gi + half
                P.op("pe", (lambda half, k8: lambda e: e.transpose(out=pty[:, k8 * 128:(k8 + 1) * 128], in_=yn[:, half * 128:(half + 1) * 128], identity=ident_b[:]))(half, k8),
                     reads=[ryn, r_const], writes=[rpty])
        P.op("act", lambda e: e.activation(out=mix[:, 4:12, cs_], in_=pty[:, :].rearrange("p (k t) -> p k t", k=8), func=AF.Copy),
             reads=[rpty], writes=[r_mix[k][c] for k in range(4, 12)])
        for gi in range(4):
            Xd, rXd = xd256.get()
            P.op("dve", (lambda gi: lambda e: e.tensor_tensor(out=Xd[:].rearrange("p (r q) -> p r q", r=4),
                                                               in0=Xdt[:, gi * 256:(gi + 1) * 256].rearrange("p (r q) -> p r q", r=4),
                                                               in1=ee[:, 4 + 4 * gi:8 + 4 * gi].unsqueeze(2).to_broadcast([128, 4, 64]), op=ALU.mult))(gi),
                 reads=[rX, ree], writes=[rXd])
            ph, rph = med.get(256)
            P.op("pe", (lambda gi: lambda e: e.matmul(ph, lhsT=Bt[:, gi * 128:(gi + 1) * 128], rhs=Xd[:], start=True, stop=True))(gi),
                 reads=[rBt, rXd], writes=[rph])
            P.op("dve", (lambda gi: lambda e: e.tensor_tensor(out=HT[gi][:].rearrange("p (r q) -> p r q", r=4), in0=HT[gi][:].rearrange("p (r q) -> p r q", r=4),
                                                               in1=et[:, 4 + 4 * gi:8 + 4 * gi].unsqueeze(2).to_broadcast([128, 4, 64]), op=ALU.mult))(gi),
                 reads=[r_HT[gi], ret], writes=[r_HT[gi]])
            P.op("dve", (lambda gi: lambda e: e.tensor_tensor(out=HT[gi][:], in0=HT[gi][:], in1=ph, op=ALU.add))(gi), reads=[r_HT[gi], rph], writes=[r_HT[gi]])
            refresh_s(gi)

    def conv_module():
        for jj in range(4):
            cw = lambda t, jj=jj: cols[:, C_C31W + jj * 31 + t:C_C31W + jj * 31 + t + 1]
            P.op("dve", (lambda jj, cw: lambda e: e.tensor_scalar(out=cacc[:, jj, :], in0=ubuf[:, jj, 0:TW], scalar1=cw(0),
                                                                  scalar2=cols[:, C_C31B + jj:C_C31B + jj + 1], op0=ALU.mult, op1=ALU.add))(jj, cw),
                 reads=[r_ub[jj], r_cols], writes=[r_cacc[jj]])
            for t in range(1, 31):
                P.op("dve", (lambda jj, t, cw: lambda e: e.scalar_tensor_tensor(out=cacc[:, jj, :], in0=ubuf[:, jj, t:t + TW], scalar=cw(t), in1=cacc[:, jj, :],
                                                                                 op0=ALU.mult, op1=ALU.add))(jj, t, cw),
                     reads=[r_ub[jj], r_cacc[jj], r_cols], writes=[r_cacc[jj]])
            P.op("pool", (lambda jj: lambda e: e.tensor_copy(out=ubuf[:, jj, 0:30], in_=ubuf[:, jj, TW:TW + 30]))(jj), reads=[r_ub[jj]], writes=[r_ub[jj]])
        p1, rp1 = big.get(); p2, rp2 = big.get()
        for jj in range(4):
            P.op("pe", (lambda jj: lambda e: e.matmul(p1[:, :], lhsT=ones_f[:], rhs=cacc[:, jj, :], start=(jj == 0), stop=(jj == 3)))(jj),
                 reads=[r_cacc[jj], r_const], writes=[rp1])
        for jj in range(4):
            sq, rsq = f512.get()
            P.op("act", (lambda jj: lambda e: e.activation(out=sq[:, :], in_=cacc[:, jj, :], func=AF.Square))(jj), reads=[r_cacc[jj]], writes=[rsq])
            P.op("pe", (lambda jj: lambda e: e.matmul(p2[:, :], lhsT=ones_f[:], rhs=sq[:, :], start=(jj == 0), stop=(jj == 3)))(jj),
                 reads=[rsq, r_const], writes=[rp2])
        mean, rmean = f512.get()
        P.op("act", lambda e: e.activation(out=mean[:, :], in_=p1[:, :], func=AF.Copy, scale=1.0 / 512), reads=[rp1], writes=[rmean])
        var, rvar = f512.get()
        P.op("dve", lambda e: e.tensor_tensor(out=var[:, :], in0=mean[:, :], in1=mean[:, :], op=ALU.mult), reads=[rmean], writes=[rvar])
        P.op("dve", lambda e: e.scalar_tensor_tensor(out=var[:, :], in0=p2[:, :], scalar=1.0 / 512, in1=var[:, :], op0=ALU.mult, op1=ALU.subtract),
             reads=[rp2, rvar], writes=[rvar])
        P.op("act", lambda e: e.activation(out=var[:, :], in_=var[:, :], func=AF.Sqrt, bias=LN_EPS), reads=[rvar], writes=[rvar])
        P.op("dve", lambda e: e.reciprocal(out=var[:, :], in_=var[:, :]), reads=[rvar], writes=[rvar])
        for jj in range(4):
            P.op("dve", (lambda jj: lambda e: e.tensor_tensor(out=cacc[:, jj, :], in0=cacc[:, jj, :], in1=mean[:, :], op=ALU.subtract))(jj),
                 reads=[r_cacc[jj], rmean], writes=[r_cacc[jj]])
            P.op("dve", (lambda jj: lambda e: e.tensor_tensor(out=cacc[:, jj, :], in0=cacc[:, jj, :], in1=var[:, :], op=ALU.mult))(jj),
                 reads=[r_cacc[jj], rvar], writes=[r_cacc[jj]])
            P.op("act", (lambda jj: lambda e: e.activation(out=mix[:, 12 + jj, :], in_=cacc[:, jj, :], func=AF.Silu,
                                                           scale=cols[:, C_LNG + jj:C_LNG + jj + 1], bias=cols[:, C_LNB + jj:C_LNB + jj + 1]))(jj),
                 reads=[r_cacc[jj], r_cols], writes=r_mix[12 + jj])

    def outproj():
        for m in range(8):
            op_, ro = opan.get()
            P.dma("sp", ro.track, (lambda m: lambda e: e.dma_start(out=op_[:].rearrange("p k c -> p (k c)"), in_=w_bf["wout"][m]))(m),
                  reads=[w_res["wout"][m]], writes=[ro])
            po, rpo = big.get()
            for kc in range(16):
                P.op("pe", (lambda kc: lambda e: e.matmul(po[:, :], lhsT=op_[:, kc, :], rhs=mix[:, kc, :], start=(kc == 0), stop=(kc == 15)))(kc),
                     reads=[ro] + r_mix[kc], writes=[rpo])
            P.op("dve", (lambda m: lambda e: e.tensor_tensor(out=xT[:, m, :], in0=xT[:, m, :], in1=po[:, :], op=ALU.add))(m),
                 reads=[rpo, r_xT[m]], writes=[r_xT[m]])

    def store_out(ti):
        if final_norm:
            ps, rp = big.get()
            for kc in range(8):
                sq, rs = f512.get()
                P.op("act", (lambda kc: lambda e: e.activation(out=sq[:, :], in_=xT[:, kc, :], func=AF.Square))(kc), reads=[r_xT[kc]], writes=[rs])
                P.op("pe", (lambda kc: lambda e: e.matmul(ps[:, :], lhsT=ones_f[:], rhs=sq[:, :], start=(kc == 0), stop=(kc == 7)))(kc),
                     reads=[rs, r_const], writes=[rp])
            P.op("act", lambda e: e.activation(out=rstd_t[:, :], in_=ps[:, :], func=AF.Sqrt, scale=1.0 / D, bias=RMS_EPS), reads=[rp], writes=[r_rstd])
            P.op("dve", lambda e: e.reciprocal(out=rstd_t[:, :], in_=rstd_t[:, :]), reads=[r_rstd], writes=[r_rstd])
            for kc in range(8):
                P.op("dve", (lambda kc: lambda e: e.scalar_tensor_tensor(out=xT[:, kc, :], in0=xT[:, kc, :], scalar=cols[:, C_GF + kc:C_GF + kc + 1],
                                                                          in1=rstd_t[:, :], op0=ALU.mult, op1=ALU.mult))(kc),
                     reads=[r_xT[kc], r_rstd, r_cols], writes=[r_xT[kc]])
        for sbk in range(4):
            xo, rxo = xin.get()
            for half in range(2):
                pt, rp = big.get()
                for j in range(4):
                    kc = half * 4 + j
                    P.op("pe", (lambda kc, j: lambda e: e.transpose(out=pt[:, j * 128:(j + 1) * 128], in_=xT[:, kc, sbk * 128:(sbk + 1) * 128], identity=ident_f[:]))(kc, j),
                         reads=[r_xT[kc], r_const], writes=[rp])
                P.op("act", (lambda half: lambda e: e.activation(out=xo[:, half * 512:(half + 1) * 512], in_=pt[:, :], func=AF.Copy))(half), reads=[rp], writes=[rxo])
            r0 = ti * TW + sbk * 128
            P.dma("pool", "st_" + rxo.track, (lambda r0: lambda e: e.dma_start(out=y_d[r0:r0 + 128, :], in_=xo[:, :]))(r0), reads=[rxo])

    negA = sb("negA", [128, 16], F32); r_negA = Res("negA")
    P.op("act", lambda e: e.activation(out=negA[:], in_=rows[:, R_ALOG:R_ALOG + 16], func=AF.Exp), reads=[r_rows], writes=[r_negA])
    P.op("dve", lambda e: e.tensor_scalar(out=negA[:], in0=negA[:], scalar1=-1.0, scalar2=None, op0=ALU.mult), reads=[r_negA], writes=[r_negA])

    STAGE = int(os.environ.get("K_STAGE", "99"))
    load_x(0, HALO, 0)
    if STAGE >= 2:
        ffn(HALO, "gu1", "d1", C_G1)
    if STAGE >= 3:
        rms_to_hbf(HALO, C_GM)
        inproj_feature(HALO, True)

    for ti in range(ntiles):
        for sbk in range(4):
            load_x(HALO + ti * TW + sbk * 128, 128, sbk * 128)
        if STAGE >= 2:
            ffn(TW, "gu1", "d1", C_G1)
        if STAGE >= 3:
            rms_to_hbf(TW, C_GM)
            inproj_feature(TW, False)
            inproj_token()
        if STAGE >= 4:
            conv4()
        for c in range(4):
            if STAGE >= 5:
                g = gate_prep(c)
                mlstm_chunk(c, g)
            if STAGE >= 6:
                ssd_chunk(c, g)
        if STAGE >= 7:
            conv_module()
            outproj()
        if STAGE >= 8:
            ffn(TW, "gu2", "d2", C_G2)
        store_out(ti)

    for h in range(4):
        P.dma("pool", "stout", (lambda hh: lambda e: e.dma_start(out=st_m_out[hh], in_=CT[hh][:]))(h), reads=[r_CT[h]])
        P.dma("pool", "stout", (lambda hh: lambda e: e.dma_start(out=st_s_out[hh], in_=HT[hh][:]))(h), reads=[r_HT[h]])
    P.emit()
    return nc


def _gu(wg, wu):
    g = wg.reshape(8, 128, NF, 128).transpose(2, 1, 0, 3)
    u = wu.reshape(8, 128, NF, 128).transpose(2, 1, 0, 3)
    return np.ascontiguousarray(np.stack([g, u], axis=2).reshape(NF, 128, 2048))


def _dn(wd):
    return np.ascontiguousarray(wd.reshape(NF, 128, 8, 128).transpose(2, 1, 0, 3).reshape(8, 128, NF * 128))


def _panels(w):
    n = w.shape[1] // 128
    return w.reshape(8, 128, n, 128).transpose(2, 1, 0, 3).reshape(n, 128, 1024)


def layer_inputs(inp, l):
    w_in = inp["w_in"][l]
    q, k, v, o = w_in[:, 0:512], w_in[:, 512:1024], w_in[:, 1024:1536], w_in[:, 1536:2048]
    gi, gf = w_in[:, 2048:2052], w_in[:, 2052:2056]
    z = w_in[:, 2056:3080]
    xs, Bm, Cm = w_in[:, 3080:4104], w_in[:, 4104:4616], w_in[:, 4616:5128]
    dt = w_in[:, 5128:5144]
    ga, gb = w_in[:, 5144:5656], w_in[:, 5656:6168]
    gab = np.stack([ga.reshape(1024, 4, 128), gb.reshape(1024, 4, 128)], axis=2).reshape(1024, 1024)
    winf = np.ascontiguousarray(_panels(np.concatenate([q, k, o, xs, Bm, Cm, gab], axis=1)))
    small = np.concatenate([gi, gf, dt, np.zeros((1024, 256 - 24), np.float32)], axis=1)
    wtok = np.ascontiguousarray(np.stack([a.reshape(8, 128, 256).transpose(1, 0, 2).reshape(128, 2048)
                                          for a in (v[:, :256], v[:, 256:], z[:, :256], z[:, 256:512], z[:, 512:768], z[:, 768:], small)]))
    wout = np.ascontiguousarray(inp["w_out"][l].reshape(16, 128, 8, 128).transpose(2, 1, 0, 3).reshape(8, 128, 2048))
    col = lambda a: a.reshape(-1, 128).T
    cw4 = np.concatenate([inp["mlstm_conv_w"][l], inp["ssd_conv_w"][l]], axis=1)
    cb4 = np.concatenate([inp["mlstm_conv_b"][l], inp["ssd_conv_b"][l]])
    cw4c = cw4.reshape(4, 24, 128).transpose(2, 1, 0).reshape(128, 96)
    c31 = inp["cm_conv_w"][l].reshape(31, 4, 128).transpose(2, 1, 0).reshape(128, 124)
    cols = np.concatenate([col(inp["ffn1_norm"][l]), col(inp["mix_norm"][l]), col(inp["ffn2_norm"][l]), col(inp["final_norm"]),
                           cw4c, col(cb4), col(inp["mlstm_norm"][l]), c31, col(inp["cm_conv_b"][l]), col(inp["cm_ln_g"][l]),
                           col(inp["cm_ln_b"][l])], axis=1)
    cols = np.ascontiguousarray(np.concatenate([cols, np.zeros((128, 320 - cols.shape[1]), np.float32)], axis=1))
    rows = np.concatenate([inp["mlstm_gate_b"][l], inp["ssd_dt_bias"][l], inp["ssd_a_log"][l], inp["ssd_d"][l], inp["ssd_norm"][l]])
    rows = np.ascontiguousarray(np.concatenate([rows, np.zeros(1088 - rows.shape[0], np.float32)])[None, :])
    return {
        "gu1": _gu(inp["ffn1_w_gate"][l], inp["ffn1_w_up"][l]), "d1": _dn(inp["ffn1_w_down"][l]),
        "gu2": _gu(inp["ffn2_w_gate"][l], inp["ffn2_w_up"][l]), "d2": _dn(inp["ffn2_w_down"][l]),
        "winf": winf, "wtok": wtok, "wout": wout, "cols": cols, "rows": rows,
    }


_NC_CACHE = {}


def _get_nc(NT, final):
    key = (NT, final)
    if key not in _NC_CACHE:
        _NC_CACHE[key] = build_layer(NT, final)
    return _NC_CACHE[key]


def kernel(**inputs):
    inp = {k: np.asarray(v, dtype=np.float32) for k, v in inputs.items()}
    x = inp["x"]
    B, S, _ = x.shape
    nseg = NSEG
    NT = S // nseg
    cur = x
    for l in range(2):
        lw = layer_inputs(inp, l)
        nc = _get_nc(NT, l == 1)
        out = np.zeros_like(cur)
        st_m = [np.zeros((4, 128, 129), np.float32) for _ in range(B)]
        st_s = [np.zeros((4, 128, 256), np.float32) for _ in range(B)]
        for seg in range(nseg):
            in_maps = []
            for c in range(NCORES):
                b, sg = c // nseg, c % nseg
                if sg == 0:
                    halo = np.zeros((HALO, D), np.float32)
                else:
                    halo = cur[b, sg * NT - HALO:sg * NT]
                m = dict(lw)
                m["x"] = np.ascontiguousarray(np.concatenate([halo, cur[b, sg * NT:(sg + 1) * NT]], axis=0))
                m["st_m_in"] = st_m[b]
                m["st_s_in"] = st_s[b]
                in_maps.append(m)
            res = run_bass_kernel_spmd(nc, in_maps, core_ids=list(range(NCORES)))
            for b in range(B):
                r = res.results[b * nseg + seg]
                out[b, seg * NT:(seg + 1) * NT] = r["y"]
                st_m[b] = np.asarray(r["st_m_out"]); st_s[b] = np.asarray(r["st_s_out"])
        cur = out
    return cur
```

```python
import math
import os
import types
from contextlib import ExitStack
import numpy as np
import concourse.bass as bass
import concourse.mybir as mybir
from concourse.bass_utils import run_bass_kernel_spmd

F32 = mybir.dt.float32
BF16 = mybir.dt.bfloat16
AF = mybir.ActivationFunctionType
ALU = mybir.AluOpType

D = 1024
DFF = 2816
NF = 22
NCORES = 8
SEQ = 16384
NSEG = 4
NT_FULL = SEQ // NSEG
HALO = 32
TW = 512
RMS_EPS = 1e-6
LN_EPS = 1e-5
LNSCALE = math.log(128 ** -0.5)
NEG = -30000.0


class Res:
    __slots__ = ("name", "writer", "readers", "track")

    def __init__(self, name):
        self.name = name
        self.track = name
        self.writer = None
        self.readers = []


class Op:
    __slots__ = ("queue", "track", "fn", "deps", "needs_inc", "semval", "is_dma", "inc")


SYNC_SAME = set(os.environ.get('K_SYNC_SAME', 'pool,act,dve').split(','))


class Prog:
    QUEUES = ("pe", "dve", "act", "pool", "sp")

    def __init__(self, nc):
        self.nc = nc
        self.ops = {q: [] for q in self.QUEUES}
        self.tracks = {}

    @staticmethod
    def _freeze(fn):
        if fn.__closure__ is None:
            return fn
        cells = []
        for c in fn.__closure__:
            try:
                cells.append(types.CellType(c.cell_contents))
            except ValueError:
                cells.append(c)
        return types.FunctionType(fn.__code__, fn.__globals__, fn.__name__, fn.__defaults__, tuple(cells))

    def _add(self, queue, track, fn, reads, writes, is_dma, inc=None):
        fn = self._freeze(fn)
        op = Op()
        op.inc = inc if inc is not None else (16 if is_dma else 1)
        op.queue, op.track, op.fn, op.is_dma = queue, track, fn, is_dma
        op.needs_inc = is_dma
        op.deps = []
        deps = {}
        for r in reads:
            if r.writer is not None:
                deps[id(r.writer)] = r.writer
        for w in writes:
            if w.writer is not None:
                deps[id(w.writer)] = w.writer
            for rd in w.readers:
                deps[id(rd)] = rd
        for d in deps.values():
            if (not d.is_dma) and d.queue == queue and not is_dma and queue not in SYNC_SAME:
                continue
            op.deps.append(d)
            d.needs_inc = True
        self.ops[queue].append(op)
        self.tracks.setdefault(track, []).append(op)
        for r in reads:
            r.readers.append(op)
        for w in writes:
            w.writer = op
            w.readers = []
        return op

    def op(self, queue, fn, reads=(), writes=()):
        return self._add(queue, queue, fn, reads, writes, False)

    def dma(self, queue, track, fn, reads=(), writes=(), inc=None):
        return self._add(queue, "dma_" + track, fn, reads, writes, True, inc)

    def emit(self):
        nc = self.nc
        for tname, tops in self.tracks.items():
            v = 0
            for o in tops:
                if o.needs_inc:
                    v += o.inc
                    o.semval = v
                else:
                    o.semval = None
        finals = {}
        for t, tops in self.tracks.items():
            if t.startswith("dma_"):
                finals.setdefault(tops[-1].queue, []).append(tops[-1])
        with ExitStack() as st:
            sems = {t: st.enter_context(nc.semaphore("s_" + t)) for t in self.tracks}
            block = st.enter_context(nc.Block())
            handles = {"pe": block.tensor, "dve": block.vector, "act": block.scalar,
                       "pool": block.gpsimd, "sp": block.sync}

            def make(q):
                def body(eng):
                    waited = {}
                    for o in self.ops[q]:
                        need = {}
                        for d in o.deps:
                            if need.get(d.track, 0) < d.semval:
                                need[d.track] = d.semval
                        for t, v in need.items():
                            if waited.get(t, 0) < v:
                                eng.wait_ge(sems[t], v)
                                waited[t] = v
                        ins = o.fn(eng)
                        if o.needs_inc:
                            ins.then_inc(sems[o.track], o.inc)
                    for o in finals.get(q, []):
                        eng.wait_ge(sems[o.track], o.semval)
                return body

            for q in self.QUEUES:
                handles[q](make(q))


class Rot:
    def __init__(self, nc, name, n, shape, dtype, psum=False):
        self.bufs = []
        for i in range(n):
            if psum:
                t = nc.alloc_psum_tensor(f"{name}{i}", shape, dtype)
            else:
                t = nc.alloc_sbuf_tensor(f"{name}{i}", shape, dtype)
            self.bufs.append((t, Res(f"{name}{i}")))
        self.i = 0

    def get(self):
        b = self.bufs[self.i % len(self.bufs)]
        self.i += 1
        return b


class SubRot:
    def __init__(self, nc, name, nbanks, per_bank, width, dtype):
        self.bufs = []
        tot = 512 if dtype == F32 else 1024
        for b in range(nbanks):
            t = nc.alloc_psum_tensor(f"{name}{b}", [128, tot], dtype)
            for j in range(per_bank):
                self.bufs.append((t, j * (tot // per_bank), Res(f"{name}{b}_{j}")))
        self.i = 0
        self.width = width

    def get(self, w=None):
        t, off, r = self.bufs[self.i % len(self.bufs)]
        self.i += 1
        w = self.width if w is None else w
        return t[:, off:off + w], r


PAN_Q, PAN_K, PAN_O, PAN_XS, PAN_B, PAN_C, PAN_G = 0, 4, 8, 12, 20, 24, 28
NPAN = 36


def build_fused(NT):
    nc = bass.Bass("TRN2", target_bir_lowering=False)
    P = Prog(nc)
    ntiles = NT // TW
    final_norm = False

    def din(name, shape, dt=F32):
        return nc.dram_tensor(name, list(shape), dt, kind="ExternalInput").ap()

    def dout(name, shape, dt=F32):
        return nc.dram_tensor(name, list(shape), dt, kind="ExternalOutput").ap()

    def dscr(name, shape, dt=BF16):
        return nc.dram_tensor(name, list(shape), dt).ap()

    x_d = din("x", [HALO + NT, D])
    y_d = dout("y", [NT, D])
    WSHAPES = {"gu1": [NF, 128, 2048], "d1": [8, 128, NF * 128], "gu2": [NF, 128, 2048], "d2": [8, 128, NF * 128],
               "winf": [NPAN, 128, 1024], "wtok": [7, 128, 2048], "wout": [8, 128, 2048]}
    w_f32_l = [{k: din(f"{k}_{l}", shp) for k, shp in WSHAPES.items()} for l in range(2)]
    w_bf_l = [{k: dscr(f"{k}_{l}_bf", shp) for k, shp in WSHAPES.items()} for l in range(2)]
    w_res_l = [{k: [Res(f"{k}_{l}_{i}") for i in range(shp[0])] for k, shp in WSHAPES.items()} for l in range(2)]
    cols_d_l = [din(f"cols_{l}", [128, 320]) for l in range(2)]
    rows_d_l = [din(f"rows_{l}", [1, 1088]) for l in range(2)]
    masks_d = din("masks", [1, 16])
    x1s = dscr("x1s", [8, 128, NT], F32); r_x1s = [Res(f"x1s{t}") for t in range(ntiles)]
    xs1 = dscr("xs1", [8, 128, NT], F32); r_xs1 = [Res(f"xs1{t}") for t in range(ntiles)]
    NEXP = 516 + 1024 + 20
    exp_d = [dscr(f"exp{l}", [128, NEXP], F32) for l in range(2)]; r_exp = [Res(f"exp{l}") for l in range(2)]
    gath_d = [dscr(f"gath{l}", [NCORES * 128, NEXP], F32) for l in range(2)]; r_gath = [Res(f"gath{l}") for l in range(2)]
    tail_d = dscr("tail", [128, 256], F32); r_tail = Res("tail")
    tailg_d = dscr("tailg", [NCORES * 128, 256], F32); r_tailg = Res("tailg")

    sb = nc.alloc_sbuf_tensor

    ident_f = sb("ident_f", [128, 128], F32); ident_b = sb("ident_b", [128, 128], BF16)
    ones_f = sb("ones_f", [128, 128], F32); ones_b = sb("ones_b", [128, 128], BF16)
    tri_f = sb("tri_f", [128, 128], F32); gt_f = sb("gt_f", [128, 128], F32); negI = sb("negI", [128, 128], F32)
    r_const = Res("const")
    cols = sb("cols_sb", [128, 320], F32); rows = sb("rows_sb", [128, 1088], F32)
    r_cols = Res("cols"); r_rows = Res("rows")

    def cst(q, fn):
        P.op(q, fn, writes=[r_const])
    cst("pool", lambda e: e.memset(ones_f[:], 1.0))
    cst("pool", lambda e: e.memset(ones_b[:], 1.0))
    cst("pool", lambda e: e.memset(ident_f[:], 1.0))
    cst("pool", lambda e: e.affine_select(out=ident_f[:], in_=ident_f[:], pattern=[[-1, 128]], compare_op=ALU.is_equal,
                                          fill=0.0, base=0, channel_multiplier=1))
    cst("pool", lambda e: e.tensor_copy(out=ident_b[:], in_=ident_f[:]))
    cst("pool", lambda e: e.memset(tri_f[:], 1.0))
    cst("pool", lambda e: e.affine_select(out=tri_f[:], in_=tri_f[:], pattern=[[1, 128]], compare_op=ALU.is_ge,
                                          fill=0.0, base=0, channel_multiplier=-1))
    cst("pool", lambda e: e.memset(gt_f[:], 1.0))
    cst("pool", lambda e: e.affine_select(out=gt_f[:], in_=gt_f[:], pattern=[[-1, 128]], compare_op=ALU.is_gt,
                                          fill=0.0, base=0, channel_multiplier=1))
    cst("pool", lambda e: e.tensor_scalar(out=negI[:], in0=ident_f[:], scalar1=NEG, scalar2=None, op0=ALU.mult))
    gt4 = sb("gt4", [128, 512], F32)
    for h in range(4):
        cst("pool", (lambda h: lambda e: e.tensor_copy(out=gt4[:, h * 128:(h + 1) * 128], in_=gt_f[:]))(h))
    masks = sb("masks_sb", [128, 16], F32); r_masks = Res("masks")
    P.dma("sp", "masks", lambda e: e.dma_start(out=masks[:], in_=masks_d.partition_broadcast(128)), writes=[r_masks])

    C_G1, C_GM, C_G2, C_GF = 0, 8, 16, 24
    C_CW, C_CB = 32, 128
    C_MN = 152
    C_C31W, C_C31B, C_LNG, C_LNB = 156, 280, 284, 288
    R_GB, R_DTB, R_ALOG, R_D, R_SN = 0, 8, 24, 40, 56

    tok = sb("casttok", [128, 8], F32)
    for l in range(2):
        for k in ("gu1", "d1", "winf", "wtok", "wout", "gu2", "d2"):
            r_k = Res(f"cast_{k}_{l}")
            for i in range(WSHAPES[k][0]):
                P.dma("pool", f"cast_{k}_{l}", (lambda l, kk, ii: lambda e: e.dma_start(out=w_bf_l[l][kk][ii], in_=w_f32_l[l][kk][ii]))(l, k, i), writes=[r_k])
            P.op("pool", lambda e: e.memset(tok[:], 0.0), reads=[r_k], writes=w_res_l[l][k])

    CT = [sb(f"CT{h}", [128, 129], F32) for h in range(4)]; r_CT = [Res(f"CT{h}") for h in range(4)]
    CTb = [sb(f"CTb{h}", [128, 128], BF16) for h in range(4)]; r_CTb = [Res(f"CTb{h}") for h in range(4)]
    nB = [sb(f"nB{h}", [128, 128], BF16) for h in range(4)]; r_nB = [Res(f"nB{h}") for h in range(4)]
    HT = [sb(f"HT{g}", [128, 256], F32) for g in range(4)]; r_HT = [Res(f"HT{g}") for g in range(4)]
    HTb = [sb(f"HTb{g}", [128, 256], BF16) for g in range(4)]; r_HTb = [Res(f"HTb{g}") for g in range(4)]

    def refresh_m(h):
        P.op("act", lambda e: e.activation(out=CTb[h][:], in_=CT[h][:, 0:128], func=AF.Copy), reads=[r_CT[h]], writes=[r_CTb[h]])
        P.op("act", lambda e: e.activation(out=nB[h][:], in_=ones_f[:], func=AF.Copy, scale=CT[h][:, 128:129]),
             reads=[r_CT[h], r_const], writes=[r_nB[h]])

    def refresh_s(g):
        P.op("act", lambda e: e.activation(out=HTb[g][:], in_=HT[g][:], func=AF.Copy), reads=[r_HT[g]], writes=[r_HTb[g]])
    totacc = sb("totacc", [128, 20], F32); r_totacc = Res("totacc")
    halo4s = sb("halo4s", [128, 24, 3], BF16); uhs = sb("uhs", [128, 4, 30], F32); r_hsave = Res("hsave")

    def zero_state():
        for h in range(4):
            P.op("pool", (lambda h: lambda e: e.memset(CT[h][:], 0.0))(h), writes=[r_CT[h]])
            P.op("pool", (lambda h: lambda e: e.memset(HT[h][:], 0.0))(h), writes=[r_HT[h]])
        P.op("pool", lambda e: e.memset(totacc[:], 0.0), writes=[r_totacc])

    xin = Rot(nc, "xin", 2, [128, D], F32)
    xT = sb("xT", [128, 8, TW], F32); r_xT = [Res(f"xT{k}") for k in range(8)]
    mix = sb("mix", [128, 16, TW], BF16)
    hbf = mix[:, 0:8, :]; r_hbf = [Res(f"hbf{k}") for k in range(8)]
    arena = sb("arena", [128, 24 * (TW + 3)], BF16)
    hid = arena[:, 0:NF * TW].rearrange("p (f t) -> p f t", f=NF); r_hid = [Res(f"hid{f}") for f in range(NF)]
    convin = arena[:, :].rearrange("p (j t) -> p j t", j=24); r_cin = [Res(f"cin{j}") for j in range(24)]
    r_A1 = Res("arenaA1"); r_A2 = Res("arenaA2")
    halo4 = sb("halo4", [128, 24, 3], BF16); r_h4 = [Res(f"halo4_{j}") for j in range(24)]
    cvo = sb("cvo", [128, 24, TW], BF16); r_cvo = [Res(f"cvo{j}") for j in range(24)]
    for j in range(4):
        r_cvo[2 * j + 1] = r_cvo[2 * j]
    sigo = sb("sigo", [128, 4, TW], BF16); r_sigo = [Res(f"sigo{j}") for j in range(4)]
    ubuf = sb("ubuf", [128, 4, 30 + TW], F32); r_ub = [Res(f"ub{j}") for j in range(4)]
    cacc = sb("cacc", [128, 4, TW], F32); r_cacc = [Res(f"cacc{j}") for j in range(4)]
    vtok = sb("vtok", [128, 4, 512], BF16); r_vtok = [Res(f"vtok{c}") for c in range(4)]
    zs = sb("zs", [128, 4, 1024], BF16); r_zs = [Res(f"zs{c}") for c in range(4)]
    gates = sb("gates", [128, 4, 24], F32); r_gates = [Res(f"gates{c}") for c in range(4)]
    r_mix = [[(r_hbf[k] if k < 8 else Res(f"mix{k}_{c}")) for c in range(4)] for k in range(16)]
    gupan = Rot(nc, "gupan", 2, [128, 2, 8, 128], BF16)
    dpan = Rot(nc, "dpan", 1, [128, NF, 128], BF16)
    wpan = Rot(nc, "wpan", 2, [128, 8, 128], BF16)
    tpan = Rot(nc, "tpan", 1, [128, 8, 256], BF16)
    opan = Rot(nc, "opan", 2, [128, 16, 128], BF16)
    f512 = Rot(nc, "f512", 4, [128, TW], F32)
    b512 = Rot(nc, "b512", 3, [128, TW], BF16)
    vw4p = Rot(nc, "vw4", 1, [128, 4, 129], BF16)
    f256 = Rot(nc, "f256", 4, [128, 256], F32)
    j256 = Rot(nc, "j256", 1, [128, 256], BF16)
    b256 = Rot(nc, "b256", 4, [128, 256], BF16)
    xdt = Rot(nc, "xdt", 1, [128, 1024], BF16)
    xsd = Rot(nc, "xsd", 1, [128, 1024], BF16)
    btok = Rot(nc, "btok", 1, [128, 512], BF16)
    sm = Rot(nc, "sm", 14, [128, 96], F32)
    rstd_t = sb("rstd_t", [128, TW], F32); r_rstd = Res("rstd")
    big = Rot(nc, "pbig", 3, [128, 512], F32, psum=True)
    sml = SubRot(nc, "psml", 3, 1, 128, F32)
    med = sml
    ptr = Rot(nc, "ptr", 2, [128, 1024], BF16, psum=True)
    kt4p = Rot(nc, "kt4", 1, [128, 512], BF16)
    xd256 = Rot(nc, "xd256", 1, [128, 256], BF16)

    def load_x(row0, n, dst_c0):
        xt, rx = xin.get()
        P.dma("pool", rx.track, lambda e: e.dma_start(out=xt[0:n, :], in_=x_d[row0:row0 + n, :]), writes=[rx])
        for half in range(2):
            pt, rp = big.get()
            for j in range(4):
                kc = half * 4 + j
                P.op("pe", (lambda kc, j: lambda e: e.transpose(out=pt[:, j * 128:j * 128 + n], in_=xt[0:n, kc * 128:(kc + 1) * 128],
                                                                  identity=ident_f[0:n, 0:n]))(kc, j),
                     reads=[rx, r_const], writes=[rp])
            P.op("act", lambda e: e.activation(out=xT[:, half * 4:half * 4 + 4, dst_c0:dst_c0 + n],
                                                in_=pt[:, :].rearrange("p (j t) -> p j t", j=4)[:, :, 0:n], func=AF.Copy),
                 reads=[rp], writes=r_xT[half * 4:half * 4 + 4])

    def rms_to_hbf(TN, gcol):
        ps, rp = big.get()
        for kc in range(8):
            sq, rs = f512.get()
            P.op("act", (lambda kc: lambda e: e.activation(out=sq[:, 0:TN], in_=xT[:, kc, 0:TN], func=AF.Square))(kc),
                 reads=[r_xT[kc]], writes=[rs])
            P.op("pe", (lambda kc: lambda e: e.matmul(ps[:, 0:TN], lhsT=ones_f[:], rhs=sq[:, 0:TN], start=(kc == 0), stop=(kc == 7)))(kc),
                 reads=[rs, r_const], writes=[rp])
        P.op("act", lambda e: e.activation(out=rstd_t[:, 0:TN], in_=ps[:, 0:TN], func=AF.Sqrt, scale=1.0 / D, bias=RMS_EPS),
             reads=[rp], writes=[r_rstd])
        P.op("dve", lambda e: e.reciprocal(out=rstd_t[:, 0:TN], in_=rstd_t[:, 0:TN]), reads=[r_rstd], writes=[r_rstd])
        for kc in range(8):
            P.op("dve", (lambda kc: lambda e: e.scalar_tensor_tensor(out=hbf[:, kc, 0:TN], in0=xT[:, kc, 0:TN],
                                                                      scalar=cols[:, gcol + kc:gcol + kc + 1], in1=rstd_t[:, 0:TN],
                                                                      op0=ALU.mult, op1=ALU.mult))(kc),
                 reads=[r_xT[kc], r_rstd, r_cols], writes=[r_hbf[kc]])

    def ffn(TN, gu_key, d_key, gcol):
        rms_to_hbf(TN, gcol)
        for f in range(NF):
            gp, rg = gupan.get()
            P.dma("sp", rg.track, (lambda f: lambda e: e.dma_start(out=gp[:].rearrange("p a k c -> p (a k c)"), in_=w_bf[gu_key][f]))(f),
                  reads=[w_res[gu_key][f]], writes=[rg])
            pg, rpg = big.get()
            pu, rpu = big.get()
            for a, (pp, rpp) in enumerate(((pg, rpg), (pu, rpu))):
                for kc in range(8):
                    P.op("pe", (lambda a, kc, pp: lambda e: e.matmul(pp[:, 0:TN], lhsT=gp[:, a, kc, :], rhs=hbf[:, kc, 0:TN],
                                                                      start=(kc == 0), stop=(kc == 7)))(a, kc, pp),
                         reads=[rg, r_hbf[kc]], writes=[rpp])
            sg, rsg = f512.get()
            P.op("act", lambda e: e.activation(out=sg[:, 0:TN], in_=pg[:, 0:TN], func=AF.Silu), reads=[rpg], writes=[rsg])
            P.op("dve", (lambda f: lambda e: e.tensor_tensor(out=hid[:, f, 0:TN], in0=sg[:, 0:TN], in1=pu[:, 0:TN], op=ALU.mult))(f),
                 reads=[rsg, rpu, r_A1] if f else [rsg, rpu], writes=[r_hid[f]] if f else [r_hid[f], r_A1])
        for m in range(8):
            dp, rd = dpan.get()
            P.dma("sp", rd.track, (lambda m: lambda e: e.dma_start(out=dp[:].rearrange("p f c -> p (f c)"), in_=w_bf[d_key][m]))(m),
                  reads=[w_res[d_key][m]], writes=[rd])
            pd, rpd = big.get()
            for f in range(NF):
                P.op("pe", (lambda f: lambda e: e.matmul(pd[:, 0:TN], lhsT=dp[:, f, :], rhs=hid[:, f, 0:TN],
                                                          start=(f == 0), stop=(f == NF - 1)))(f),
                     reads=[rd, r_hid[f], r_A2], writes=[rpd])
            P.op("dve", (lambda m: lambda e: e.scalar_tensor_tensor(out=xT[:, m, 0:TN], in0=pd[:, 0:TN], scalar=0.5, in1=xT[:, m, 0:TN],
                                                                     op0=ALU.mult, op1=ALU.add))(m),
                 reads=[rpd, r_xT[m]], writes=[r_xT[m]])

    def inproj_feature(TN, halo, subset=None):
        pga = None
        if halo or subset is not None:
            order = list(range(NPAN))
        else:
            convp = [j for j in range(NPAN) if j < PAN_O or PAN_XS <= j < PAN_G]
            units = [[j] for j in range(PAN_O, PAN_XS)] + [[PAN_G + 2 * jj, PAN_G + 2 * jj + 1] for jj in range(4)]
            order = []
            for ui, u in enumerate(units):
                order += convp[3 * ui:3 * ui + 3] + u
        for j in order:
            if halo and PAN_O <= j < PAN_XS:
                continue
            if subset is not None and j not in subset:
                continue
            wp, rw = wpan.get()
            P.dma("sp", rw.track, (lambda j: lambda e: e.dma_start(out=wp[:].rearrange("p k c -> p (k c)"), in_=w_bf["winf"][j]))(j),
                  reads=[w_res["winf"][j]], writes=[rw])
            pp, rp = big.get()
            for kc in range(8):
                P.op("pe", (lambda kc: lambda e: e.matmul(pp[:, 0:TN], lhsT=wp[:, kc, :], rhs=hbf[:, kc, 0:TN],
                                                           start=(kc == 0), stop=(kc == 7)))(kc),
                     reads=[rw, r_hbf[kc]], writes=[rp])
            if j < PAN_O or PAN_XS <= j < PAN_G:
                cj = j if j < PAN_O else 8 + (j - PAN_XS)
                if halo:
                    P.op("act", (lambda cj: lambda e: e.activation(out=halo4[:, cj, :], in_=pp[:, TN - 3:TN], func=AF.Copy))(cj),
                         reads=[rp], writes=[r_h4[cj]])
                else:
                    P.op("pool", (lambda cj: lambda e: e.tensor_copy(out=convin[:, cj, 0:3], in_=halo4[:, cj, :]))(cj),
                         reads=[r_h4[cj], r_A1], writes=[r_cin[cj], r_A2])
                    P.op("act", (lambda cj: lambda e: e.activation(out=convin[:, cj, 3:3 + TN], in_=pp[:, 0:TN], func=AF.Copy))(cj),
                         reads=[rp, r_A1], writes=[r_cin[cj], r_A2])
                    P.op("pool", (lambda cj: lambda e: e.tensor_copy(out=halo4[:, cj, :], in_=convin[:, cj, TW:TW + 3]))(cj),
                         reads=[r_cin[cj], r_A1], writes=[r_h4[cj]])
                    conv4_tile(cj)
            elif j < PAN_XS:
                oj = j - PAN_O
                P.op("act", (lambda oj: lambda e: e.activation(out=sigo[:, oj, 0:TN], in_=pp[:, 0:TN], func=AF.Sigmoid))(oj),
                     reads=[rp], writes=[r_sigo[oj]])
            else:
                jj, isb = (j - PAN_G) // 2, (j - PAN_G) % 2
                if not isb:
                    pga = (pp, rp)
                else:
                    sg, rsg = f512.get()
                    P.op("act", lambda e: e.activation(out=sg[:, 0:TN], in_=pp[:, 0:TN], func=AF.Sigmoid), reads=[rp], writes=[rsg])
                    pa, rpa = pga
                    if halo:
                        P.op("dve", (lambda jj, pa: lambda e: e.tensor_tensor(out=ubuf[:, jj, 0:30], in0=pa[:, TN - 30:TN], in1=sg[:, TN - 30:TN], op=ALU.mult))(jj, pa),
                             reads=[rpa, rsg], writes=[r_ub[jj]])
                    else:
                        P.op("dve", (lambda jj, pa: lambda e: e.tensor_tensor(out=ubuf[:, jj, 30:30 + TN], in0=pa[:, 0:TN], in1=sg[:, 0:TN], op=ALU.mult))(jj, pa),
                             reads=[rpa, rsg], writes=[r_ub[jj]])
        if not halo and subset is None:
            conv31_enqueue()

    def inproj_token(subset=None):
        for pi in range(7):
            if subset is not None and pi not in subset:
                continue
            tp, rt = tpan.get()
            ncol = 256 if pi < 6 else 24
            P.dma("sp", rt.track, (lambda pi: lambda e: e.dma_start(out=tp[:].rearrange("p k c -> p (k c)"), in_=w_bf["wtok"][pi]))(pi),
                  reads=[w_res["wtok"][pi]], writes=[rt])
            for c in range(4):
                pp, rp = med.get(256)
                for kc in range(8):
                    P.op("pe", (lambda kc, c: lambda e: e.matmul(pp[:, 0:ncol], lhsT=hbf[:, kc, c * 128:(c + 1) * 128], rhs=tp[:, kc, 0:ncol],
                                                                  start=(kc == 0), stop=(kc == 7)))(kc, c),
                         reads=[rt, r_hbf[kc]], writes=[rp])
                if pi < 2:
                    P.op("act", (lambda c, pi: lambda e: e.activation(out=vtok[:, c, pi * 256:(pi + 1) * 256], in_=pp, func=AF.Copy))(c, pi),
                         reads=[rp], writes=[r_vtok[c]])
                elif pi < 6:
                    P.op("act", (lambda c, pi: lambda e: e.activation(out=zs[:, c, (pi - 2) * 256:(pi - 1) * 256], in_=pp, func=AF.Silu))(c, pi),
                         reads=[rp], writes=[r_zs[c]])
                else:
                    P.op("dve", (lambda c: lambda e: e.tensor_copy(out=gates[:, c, :], in_=pp[:, 0:24]))(c), reads=[rp], writes=[r_gates[c]])

    def conv4_tile(j):
        acc, ra = f512.get()
        cw = lambda t, j=j: cols[:, C_CW + j * 4 + t:C_CW + j * 4 + t + 1]
        P.op("dve", (lambda j, cw: lambda e: e.tensor_scalar(out=acc[:, :], in0=convin[:, j, 0:TW], scalar1=cw(0), scalar2=None, op0=ALU.mult))(j, cw),
             reads=[r_cin[j], r_cols, r_A1], writes=[ra])
        for t in range(1, 4):
            P.op("dve", (lambda j, t, cw: lambda e: e.scalar_tensor_tensor(out=acc[:, :], in0=convin[:, j, t:t + TW], scalar=cw(t), in1=acc[:, :],
                                                                            op0=ALU.mult, op1=ALU.add))(j, t, cw),
                 reads=[r_cin[j], ra, r_A1], writes=[ra])
        P.op("act", (lambda j: lambda e: e.activation(out=cvo[:, j, :], in_=acc[:, :], func=AF.Silu, bias=cols[:, C_CB + j:C_CB + j + 1]))(j),
             reads=[ra, r_cols], writes=[r_cvo[j]])

    fillers = []
    USE_FILL = False

    def fill(n):
        if not USE_FILL and n < 1000:
            return
        for _ in range(min(n, len(fillers))):
            fillers.pop(0)()

    def conv31_enqueue():
        def mk(jj, t):
            cw = cols[:, C_C31W + jj * 31 + t:C_C31W + jj * 31 + t + 1]
            if t == 0:
                return lambda: P.op("dve", lambda e: e.tensor_scalar(out=cacc[:, jj, :], in0=ubuf[:, jj, 0:TW], scalar1=cw,
                                                                     scalar2=cols[:, C_C31B + jj:C_C31B + jj + 1], op0=ALU.mult, op1=ALU.add),
                                    reads=[r_ub[jj], r_cols], writes=[r_cacc[jj]])
            return lambda: P.op("dve", lambda e: e.scalar_tensor_tensor(out=cacc[:, jj, :], in0=ubuf[:, jj, t:t + TW], scalar=cw, in1=cacc[:, jj, :],
                                                                        op0=ALU.mult, op1=ALU.add),
                                reads=[r_ub[jj], r_cacc[jj], r_cols], writes=[r_cacc[jj]])
        for t in range(31):
            for jj in range(4):
                fillers.append(mk(jj, t))

    def gate_prep_tile():
        def t3(n=24):
            t, r = sm.get()
            return t[:, :].rearrange("p (c k) -> p c k", c=4), r
        rb = lambda col0, n: rows[:, col0:col0 + n].unsqueeze(1).to_broadcast([128, 4, n])
        pre, rpre = t3()
        P.op("dve", lambda e: e.tensor_tensor(out=pre[:, :, 0:24], in0=gates[:, :, :], in1=rb(R_GB, 24), op=ALU.add),
             reads=r_gates + [r_rows], writes=[rpre])
        ex, rex = t3()
        P.op("act", lambda e: e.activation(out=ex[:, :, 4:8], in_=pre[:, :, 4:8], func=AF.Exp, scale=-1.0), reads=[rpre], writes=[rex])
        P.op("act", lambda e: e.activation(out=ex[:, :, 8:24], in_=pre[:, :, 8:24], func=AF.Exp), reads=[rpre], writes=[rex])
        P.op("act", lambda e: e.activation(out=ex[:, :, 4:24], in_=ex[:, :, 4:24], func=AF.Ln, bias=1.0), reads=[rex], writes=[rex])
        Lt, rL = sm.get()
        L = Lt[:, 0:80].rearrange("p (c k) -> p c k", c=4)
        P.op("dve", lambda e: e.tensor_scalar(out=L[:, :, 0:4], in0=ex[:, :, 4:8], scalar1=-1.0, scalar2=None, op0=ALU.mult), reads=[rex], writes=[rL])
        P.op("dve", lambda e: e.tensor_tensor(out=L[:, :, 4:20], in0=ex[:, :, 8:24], in1=negA[:, 0:16].unsqueeze(1).to_broadcast([128, 4, 16]), op=ALU.mult),
             reads=[rex, r_negA], writes=[rL])
        pc, rpc = sml.get(512)
        pcv = pc[:, 0:80].rearrange("p (c k) -> p c k", c=4)
        ptv = pc[:, 128:208].rearrange("p (c k) -> p c k", c=4)
        P.op("pe", lambda e: e.matmul(pc[:, 0:80], lhsT=tri_f[:], rhs=Lt[:, 0:80], start=True, stop=True), reads=[rL, r_const], writes=[rpc])
        P.op("pe", lambda e: e.matmul(pc[:, 128:208], lhsT=ones_f[:], rhs=Lt[:, 0:80], start=True, stop=True), reads=[rL, r_const], writes=[rpc])
        cs, rcs = t3()
        P.op("dve", lambda e: e.tensor_copy(out=cs[:, :, 0:20], in_=pcv[:, :, 0:20]), reads=[rpc], writes=[rcs])
        tot, rtot = t3()
        P.op("dve", lambda e: e.tensor_copy(out=tot[:, :, 0:20], in_=ptv[:, :, 0:20]), reads=[rpc], writes=[rtot])
        wb, rwb = t3()
        P.op("dve", lambda e: e.scalar_tensor_tensor(out=wb[:, :, 0:4], in0=pre[:, :, 0:4], scalar=LNSCALE, in1=cs[:, :, 0:4], op0=ALU.add, op1=ALU.subtract),
             reads=[rpre, rcs], writes=[rwb])
        P.op("dve", lambda e: e.tensor_scalar(out=wb[:, :, 4:20], in0=cs[:, :, 4:20], scalar1=-1.0, scalar2=None, op0=ALU.mult), reads=[rcs], writes=[rwb])
        ee, ree = t3()
        P.op("dve", lambda e: e.tensor_tensor(out=ee[:, :, 0:20], in0=wb[:, :, 0:20], in1=tot[:, :, 0:20], op=ALU.add), reads=[rwb, rtot], writes=[ree])
        P.op("act", lambda e: e.activation(out=ee[:, :, 0:20], in_=ee[:, :, 0:20], func=AF.Exp), reads=[ree], writes=[ree])
        et, ret = t3()
        P.op("act", lambda e: e.activation(out=et[:, :, 0:20], in_=tot[:, :, 0:20], func=AF.Exp), reads=[rtot], writes=[ret])
        ec, rec = t3()
        P.op("act", lambda e: e.activation(out=ec[:, :, 0:20], in_=cs[:, :, 0:20], func=AF.Exp), reads=[rcs], writes=[rec])
        G = dict(L=(L, rL), dt=(ex, rex), wb=(wb, rwb), ee=(ee, ree), et=(et, ret), ec=(ec, rec), tot=(tot, rtot))
        return G

    def gate_chunk(G, c):
        return {k: (t[:, c, :], r) for k, (t, r) in G.items()}

    tri4 = tri_f[:, :].unsqueeze(1).to_broadcast([128, 4, 128])

    def mlstm_chunk(c, g, pre=False):
        cs_ = slice(c * 128, (c + 1) * 128)
        L, rL = g["L"]; wb, rwb = g["wb"]; ee, ree = g["ee"]; et, ret = g["et"]
        v4 = lambda t: t[:, :].rearrange("p (h t) -> p h t", h=4)
        ptk, rptk = ptr.get()
        for h in range(4):
            P.op("pe", (lambda h: lambda e: e.transpose(out=ptk[:, h * 128:(h + 1) * 128], in_=cvo[:, 4 + h, cs_], identity=ident_b[:]))(h),
                 reads=[r_cvo[4 + h], r_const], writes=[rptk])
        kt4, rkt4 = kt4p.get()
        P.op("act", lambda e: e.activation(out=kt4[:], in_=ptk[:, 0:512], func=AF.Copy), reads=[rptk], writes=[rkt4])
        if not pre:
            rq = [r_cvo[0], r_cvo[2]]; rk = [r_cvo[4], r_cvo[6]]
            TL4, rTL = f512.get()
            P.op("dve", lambda e: e.tensor_tensor(out=v4(TL4), in0=tri4, in1=L[:, 0:4].unsqueeze(2).to_broadcast([128, 4, 128]), op=ALU.mult),
                 reads=[rL, r_const], writes=[rTL])
            fill(FILL_N)
            pb1, rpb1 = sml.get(512)
            P.op("pe", lambda e: e.matmul(pb1, lhsT=ones_f[:], rhs=TL4[:, :], start=True, stop=True), reads=[rTL, r_const], writes=[rpb1])
            Arow, rAr = f512.get()
            P.op("act", lambda e: e.activation(out=Arow[:, :], in_=pb1, func=AF.Exp), reads=[rpb1], writes=[rAr])
            pb2, rpb2 = sml.get(512)
            P.op("pe", lambda e: e.matmul(pb2, lhsT=ones_f[:], rhs=TL4[:, :], start=True, stop=False), reads=[rTL, r_const], writes=[rpb2])
            P.op("pe", lambda e: e.matmul(pb2, lhsT=negI[:], rhs=gt4[:, :], start=False, stop=True), reads=[r_const], writes=[rpb2])
            wT, rwT = f512.get()
            for h in range(4):
                P.op("act", (lambda h: lambda e: e.activation(out=wT[:, h * 128:(h + 1) * 128], in_=pb2[:, h * 128:(h + 1) * 128], func=AF.Exp, bias=wb[:, h:h + 1]))(h),
                     reads=[rpb2, rwb], writes=[rwT])
            pqk, rpqk = sml.get(512)
            for h in range(4):
                P.op("pe", (lambda h: lambda e: e.matmul(pqk[:, h * 128:(h + 1) * 128], lhsT=cvo[:, 4 + h, cs_], rhs=cvo[:, h, cs_], start=True, stop=True))(h),
                     reads=rq + rk, writes=[rpqk])
            sqk, rsqk = b512.get()
            P.op("dve", lambda e: e.tensor_tensor(out=sqk[:, :], in0=pqk, in1=wT[:, :], op=ALU.mult), reads=[rpqk, rwT], writes=[rsqk])
            qa, rqa = b512.get()
            P.op("dve", lambda e: e.tensor_tensor(out=v4(qa), in0=cvo[:, 0:4, cs_], in1=v4(Arow), op=ALU.mult), reads=rq + [rAr], writes=[rqa])
            fill(FILL_N)
            pnum, rpn = sml.get(512)
            for h in range(4):
                hs = slice(h * 128, (h + 1) * 128)
                P.op("pe", (lambda h, hs: lambda e: e.matmul(pnum[:, hs], lhsT=vtok[:, c, hs], rhs=sqk[:, hs], start=True, stop=False))(h, hs),
                     reads=[r_vtok[c], rsqk], writes=[rpn])
                P.op("pe", (lambda h, hs: lambda e: e.matmul(pnum[:, hs], lhsT=CTb[h][:], rhs=qa[:, hs], start=False, stop=True))(h, hs),
                     reads=[r_CTb[h], rqa], writes=[rpn])
            pden, rpd = sml.get(512)
            P.op("pe", lambda e: e.matmul(pden, lhsT=ones_b[:], rhs=sqk[:, :], start=True, stop=False), reads=[r_const, rsqk], writes=[rpd])
            for h in range(4):
                hs = slice(h * 128, (h + 1) * 128)
                P.op("pe", (lambda h, hs: lambda e: e.matmul(pden[:, hs], lhsT=nB[h][:], rhs=qa[:, hs], start=False, stop=(h == 3)))(h, hs),
                     reads=[r_nB[h], rqa], writes=[rpd])
            dn, rdn = f512.get()
            P.op("act", lambda e: e.activation(out=dn[:, :], in_=pden, func=AF.Abs), reads=[rpd], writes=[rdn])
            P.op("dve", lambda e: e.tensor_scalar(out=dn[:, :], in0=dn[:, :], scalar1=1.0, scalar2=None, op0=ALU.max), reads=[rdn], writes=[rdn])
            P.op("dve", lambda e: e.reciprocal(out=dn[:, :], in_=dn[:, :]), reads=[rdn], writes=[rdn])
            hg, rhg = f512.get()
            P.op("dve", lambda e: e.tensor_tensor(out=hg[:, :], in0=pnum, in1=dn[:, :], op=ALU.mult), reads=[rpn, rdn], writes=[rhg])
            P.op("dve", lambda e: e.tensor_tensor(out=v4(hg), in0=v4(hg), in1=sigo[:, 0:4, cs_], op=ALU.mult), reads=[rhg] + r_sigo, writes=[rhg])
            fill(FILL_N)
            sq, rsq = f512.get()
            P.op("act", lambda e: e.activation(out=sq[:, :], in_=hg[:, :], func=AF.Square), reads=[rhg], writes=[rsq])
            pss, rpss = sml.get(512)
            P.op("pe", lambda e: e.matmul(pss, lhsT=ones_f[:], rhs=sq[:, :], start=True, stop=True), reads=[rsq, r_const], writes=[rpss])
            rs, rrs = f512.get()
            P.op("act", lambda e: e.activation(out=rs[:, :], in_=pss, func=AF.Sqrt, scale=1.0 / 128, bias=RMS_EPS), reads=[rpss], writes=[rrs])
            P.op("dve", lambda e: e.reciprocal(out=rs[:, :], in_=rs[:, :]), reads=[rrs], writes=[rrs])
            for h in range(4):
                hs = slice(h * 128, (h + 1) * 128)
                P.op("dve", (lambda h, hs: lambda e: e.scalar_tensor_tensor(out=mix[:, h, cs_], in0=hg[:, hs], scalar=cols[:, C_MN + h:C_MN + h + 1], in1=rs[:, hs],
                                                                             op0=ALU.mult, op1=ALU.mult))(h, hs),
                     reads=[rhg, rrs, r_cols], writes=[r_mix[h][c]])
        vw, rvw = vw4p.get()
        P.op("dve", lambda e: e.tensor_tensor(out=vw[:, :, 0:128], in0=vtok[:, c, :].rearrange("p (h t) -> p h t", h=4),
                                              in1=ee[:, 0:4].unsqueeze(2).to_broadcast([128, 4, 128]), op=ALU.mult),
             reads=[r_vtok[c], ree], writes=[rvw])
        P.op("dve", lambda e: e.tensor_copy(out=vw[:, :, 128:129], in_=ee[:, 0:4].unsqueeze(2)), reads=[ree], writes=[rvw])
        for h in range(4):
            pst, rpst = med.get(256)
            P.op("pe", (lambda h: lambda e: e.matmul(pst[:, 0:129], lhsT=kt4[:, h * 128:(h + 1) * 128], rhs=vw[:, h, :], start=True, stop=True))(h), reads=[rkt4, rvw], writes=[rpst])
            P.op("dve", (lambda h: lambda e: e.scalar_tensor_tensor(out=CT[h][:], in0=CT[h][:], scalar=et[:, h:h + 1], in1=pst[:, 0:129], op0=ALU.mult, op1=ALU.add))(h),
                 reads=[r_CT[h], ret, rpst], writes=[r_CT[h]])
            if not pre:
                refresh_m(h)

    def ssd_chunk(c, g, pre=False):
        cs_ = slice(c * 128, (c + 1) * 128)
        L, rL = g["L"]; dt, rdt = g["dt"]; wb, rwb = g["wb"]; ee, ree = g["ee"]; et, ret = g["et"]; ec, rec = g["ec"]
        Xdt, rX = xdt.get(); Bt, rBt = btok.get()
        ptx, rptx = ptr.get()
        for j in range(8):
            P.op("pe", (lambda j: lambda e: e.transpose(out=ptx[:, j * 128:(j + 1) * 128], in_=cvo[:, 8 + j, cs_], identity=ident_b[:]))(j),
                 reads=[r_cvo[8 + j], r_const], writes=[rptx])
        P.op("dve", lambda e: e.tensor_tensor(out=Xdt[:].rearrange("p (r q) -> p r q", r=16), in0=ptx[:, :].rearrange("p (r q) -> p r q", r=16),
                                              in1=dt[:, 8:24].unsqueeze(2).to_broadcast([128, 16, 64]), op=ALU.mult),
             reads=[rptx, rdt], writes=[rX])
        if not pre:
            xsD, rxd = xsd.get()
            P.op("dve", lambda e: e.tensor_tensor(out=xsD[:].rearrange("p (r q) -> p r q", r=16), in0=ptx[:, :].rearrange("p (r q) -> p r q", r=16),
                                                  in1=rows[:, R_D:R_D + 16].unsqueeze(2).to_broadcast([128, 16, 64]), op=ALU.mult),
                 reads=[rptx, r_rows], writes=[rxd])
        ptb, rptb = ptr.get()
        for gi in range(4):
            P.op("pe", (lambda gi: lambda e: e.transpose(out=ptb[:, gi * 128:(gi + 1) * 128], in_=cvo[:, 16 + gi, cs_], identity=ident_b[:]))(gi),
                 reads=[r_cvo[16 + gi], r_const], writes=[rptb])
        P.op("act", lambda e: e.activation(out=Bt[:], in_=ptb[:, 0:512], func=AF.Copy), reads=[rptb], writes=[rBt])
        if not pre:
            ssd_out(c, g, Xdt, rX, xsD, rxd)
        for gi in range(4):
            Xd, rXd = xd256.get()
            P.op("dve", (lambda gi: lambda e: e.tensor_tensor(out=Xd[:].rearrange("p (r q) -> p r q", r=4),
                                                               in0=Xdt[:, gi * 256:(gi + 1) * 256].rearrange("p (r q) -> p r q", r=4),
                                                               in1=ee[:, 4 + 4 * gi:8 + 4 * gi].unsqueeze(2).to_broadcast([128, 4, 64]), op=ALU.mult))(gi),
                 reads=[rX, ree], writes=[rXd])
            ph, rph = med.get(256)
            P.op("pe", (lambda gi: lambda e: e.matmul(ph, lhsT=Bt[:, gi * 128:(gi + 1) * 128], rhs=Xd[:], start=True, stop=True))(gi),
                 reads=[rBt, rXd], writes=[rph])
            P.op("dve", (lambda gi: lambda e: e.tensor_tensor(out=HT[gi][:].rearrange("p (r q) -> p r q", r=4), in0=HT[gi][:].rearrange("p (r q) -> p r q", r=4),
                                                               in1=et[:, 4 + 4 * gi:8 + 4 * gi].unsqueeze(2).to_broadcast([128, 4, 64]), op=ALU.mult))(gi),
                 reads=[r_HT[gi], ret], writes=[r_HT[gi]])
            P.op("dve", (lambda gi: lambda e: e.tensor_tensor(out=HT[gi][:], in0=HT[gi][:], in1=ph, op=ALU.add))(gi), reads=[r_HT[gi], rph], writes=[r_HT[gi]])
            if not pre:
                refresh_s(gi)

    def ssd_out(c, g, Xdt, rX, xsD, rxd):
        cs_ = slice(c * 128, (c + 1) * 128)
        L, rL = g["L"]; wb, rwb = g["wb"]; ec, rec = g["ec"]
        py = [big.get(), big.get()]
        for half in range(2):
            P.op("pe", (lambda half: lambda e: e.matmul(py[half][0][:, :], lhsT=ident_b[:], rhs=xsD[:, half * 512:(half + 1) * 512], start=True, stop=False))(half),
                 reads=[rxd, r_const], writes=[py[half][1]])
        for gi in range(4):
            TA4, rTA = f512.get()
            P.op("dve", (lambda gi: lambda e: e.tensor_tensor(out=TA4[:, :].rearrange("p (h t) -> p h t", h=4), in0=tri4,
                                                               in1=L[:, 4 + 4 * gi:8 + 4 * gi].unsqueeze(2).to_broadcast([128, 4, 128]), op=ALU.mult))(gi),
                 reads=[rL, r_const], writes=[rTA])
            fill(FILL_N)
            pl, rpl = sml.get(512)
            P.op("pe", lambda e: e.matmul(pl, lhsT=ones_f[:], rhs=TA4[:, :], start=True, stop=False), reads=[rTA, r_const], writes=[rpl])
            P.op("pe", lambda e: e.matmul(pl, lhsT=negI[:], rhs=gt4[:, :], start=False, stop=True), reads=[r_const], writes=[rpl])
            LT4, rLT = b512.get()
            for r in range(4):
                hh = gi * 4 + r
                P.op("act", (lambda hh, r: lambda e: e.activation(out=LT4[:, r * 128:(r + 1) * 128], in_=pl[:, r * 128:(r + 1) * 128], func=AF.Exp, bias=wb[:, 4 + hh:5 + hh]))(hh, r),
                     reads=[rpl, rwb], writes=[rLT])
            pcb, rpcb = sml.get(128)
            P.op("pe", (lambda gi: lambda e: e.matmul(pcb, lhsT=cvo[:, 16 + gi, cs_], rhs=cvo[:, 20 + gi, cs_], start=True, stop=True))(gi),
                 reads=[r_cvo[16 + gi], r_cvo[20 + gi]], writes=[rpcb])
            MT4, rMT = b512.get()
            P.op("dve", lambda e: e.tensor_tensor(out=MT4[:, :].rearrange("p (h t) -> p h t", h=4), in0=pcb.unsqueeze(1).to_broadcast([128, 4, 128]),
                                                  in1=LT4[:, :].rearrange("p (h t) -> p h t", h=4), op=ALU.mult),
                 reads=[rpcb, rLT], writes=[rMT])
            fill(FILL_N)
            for r in range(4):
                hh = gi * 4 + r
                half, hq = hh // 8, hh % 8
                P.op("pe", (lambda hh, half, hq, r: lambda e: e.matmul(py[half][0][:, hq * 64:(hq + 1) * 64], lhsT=MT4[:, r * 128:(r + 1) * 128], rhs=Xdt[:, hh * 64:(hh + 1) * 64],
                                                                        start=False, stop=(hq == 7)))(hh, half, hq, r),
                     reads=[rMT, rX], writes=[py[half][1]])
        rs4, rrs4 = sm.get()
        yg_l = []
        for gi in range(4):
            poff, rpo = med.get(256)
            P.op("pe", (lambda gi: lambda e: e.matmul(poff, lhsT=cvo[:, 20 + gi, cs_], rhs=HTb[gi][:], start=True, stop=True))(gi),
                 reads=[r_cvo[20 + gi], r_HTb[gi]], writes=[rpo])
            yo, ryo = f256.get()
            P.op("dve", (lambda gi: lambda e: e.tensor_tensor(out=yo[:].rearrange("p (r q) -> p r q", r=4), in0=poff.rearrange("p (r q) -> p r q", r=4),
                                                               in1=ec[:, 4 + 4 * gi:8 + 4 * gi].unsqueeze(2).to_broadcast([128, 4, 64]), op=ALU.mult))(gi),
                 reads=[rpo, rec], writes=[ryo])
            half, off = gi // 2, (gi % 2) * 256
            P.op("dve", (lambda half, off: lambda e: e.tensor_tensor(out=yo[:], in0=yo[:], in1=py[half][0][:, off:off + 256], op=ALU.add))(half, off),
                 reads=[ryo, py[half][1]], writes=[ryo])
            P.op("dve", (lambda gi: lambda e: e.tensor_tensor(out=yo[:], in0=yo[:], in1=zs[:, c, gi * 256:(gi + 1) * 256], op=ALU.mult))(gi),
                 reads=[ryo, r_zs[c]], writes=[ryo])
            junk, rj = j256.get()
            P.op("act", (lambda gi: lambda e: e.activation(out=junk[:], in_=yo[:], func=AF.Square, accum_out=rs4[:, gi:gi + 1]))(gi),
                 reads=[ryo], writes=[rj, rrs4])
            yg_l.append((yo, ryo))
        P.op("act", lambda e: e.activation(out=rs4[:, 0:4], in_=rs4[:, 0:4], func=AF.Sqrt, scale=1.0 / 256, bias=RMS_EPS), reads=[rrs4], writes=[rrs4])
        P.op("dve", lambda e: e.reciprocal(out=rs4[:, 0:4], in_=rs4[:, 0:4]), reads=[rrs4], writes=[rrs4])
        pty, rpty = ptr.get()
        for gi in range(4):
            yo, ryo = yg_l[gi]
            yn, ryn = b256.get()
            P.op("dve", (lambda gi, yo: lambda e: e.scalar_tensor_tensor(out=yn[:], in0=yo[:], scalar=rs4[:, gi:gi + 1], in1=rows[:, R_SN + gi * 256:R_SN + (gi + 1) * 256],
                                                                          op0=ALU.mult, op1=ALU.mult))(gi, yo),
                 reads=[ryo, rrs4, r_rows], writes=[ryn])
            for half in range(2):
                k8 = 2 * gi + half
                P.op("pe", (lambda half, k8: lambda e: e.transpose(out=pty[:, k8 * 128:(k8 + 1) * 128], in_=yn[:, half * 128:(half + 1) * 128], identity=ident_b[:]))(half, k8),
                     reads=[ryn, r_const], writes=[rpty])
        P.op("act", lambda e: e.activation(out=mix[:, 4:12, cs_], in_=pty[:, :].rearrange("p (k t) -> p k t", k=8), func=AF.Copy),
             reads=[rpty], writes=[r_mix[k][c] for k in range(4, 12)])

    def conv_module():
        fill(100000)
        for jj in range(4):
            P.op("pool", (lambda jj: lambda e: e.tensor_copy(out=ubuf[:, jj, 0:30], in_=ubuf[:, jj, TW:TW + 30]))(jj), reads=[r_ub[jj]], writes=[r_ub[jj]])
        p1, rp1 = big.get(); p2, rp2 = big.get()
        for jj in range(4):
            P.op("pe", (lambda jj: lambda e: e.matmul(p1[:, :], lhsT=ones_f[:], rhs=cacc[:, jj, :], start=(jj == 0), stop=(jj == 3)))(jj),
                 reads=[r_cacc[jj], r_const], writes=[rp1])
        for jj in range(4):
            sq, rsq = f512.get()
            P.op("act", (lambda jj: lambda e: e.activation(out=sq[:, :], in_=cacc[:, jj, :], func=AF.Square))(jj), reads=[r_cacc[jj]], writes=[rsq])
            P.op("pe", (lambda jj: lambda e: e.matmul(p2[:, :], lhsT=ones_f[:], rhs=sq[:, :], start=(jj == 0), stop=(jj == 3)))(jj),
                 reads=[rsq, r_const], writes=[rp2])
        mean, rmean = f512.get()
        P.op("act", lambda e: e.activation(out=mean[:, :], in_=p1[:, :], func=AF.Copy, scale=1.0 / 512), reads=[rp1], writes=[rmean])
        var, rvar = f512.get()
        P.op("dve", lambda e: e.tensor_tensor(out=var[:, :], in0=mean[:, :], in1=mean[:, :], op=ALU.mult), reads=[rmean], writes=[rvar])
        P.op("dve", lambda e: e.scalar_tensor_tensor(out=var[:, :], in0=p2[:, :], scalar=1.0 / 512, in1=var[:, :], op0=ALU.mult, op1=ALU.subtract),
             reads=[rp2, rvar], writes=[rvar])
        P.op("act", lambda e: e.activation(out=var[:, :], in_=var[:, :], func=AF.Sqrt, bias=LN_EPS), reads=[rvar], writes=[rvar])
        P.op("dve", lambda e: e.reciprocal(out=var[:, :], in_=var[:, :]), reads=[rvar], writes=[rvar])
        for jj in range(4):
            P.op("dve", (lambda jj: lambda e: e.tensor_tensor(out=cacc[:, jj, :], in0=cacc[:, jj, :], in1=mean[:, :], op=ALU.subtract))(jj),
                 reads=[r_cacc[jj], rmean], writes=[r_cacc[jj]])
            P.op("dve", (lambda jj: lambda e: e.tensor_tensor(out=cacc[:, jj, :], in0=cacc[:, jj, :], in1=var[:, :], op=ALU.mult))(jj),
                 reads=[r_cacc[jj], rvar], writes=[r_cacc[jj]])
            P.op("act", (lambda jj: lambda e: e.activation(out=mix[:, 12 + jj, :], in_=cacc[:, jj, :], func=AF.Silu,
                                                           scale=cols[:, C_LNG + jj:C_LNG + jj + 1], bias=cols[:, C_LNB + jj:C_LNB + jj + 1]))(jj),
                 reads=[r_cacc[jj], r_cols], writes=r_mix[12 + jj])

    def outproj():
        for m in range(8):
            op_, ro = opan.get()
            P.dma("sp", ro.track, (lambda m: lambda e: e.dma_start(out=op_[:].rearrange("p k c -> p (k c)"), in_=w_bf["wout"][m]))(m),
                  reads=[w_res["wout"][m]], writes=[ro])
            po, rpo = big.get()
            for kc in range(16):
                P.op("pe", (lambda kc: lambda e: e.matmul(po[:, :], lhsT=op_[:, kc, :], rhs=mix[:, kc, :], start=(kc == 0), stop=(kc == 15)))(kc),
                     reads=[ro] + r_mix[kc], writes=[rpo])
            P.op("dve", (lambda m: lambda e: e.tensor_tensor(out=xT[:, m, :], in0=xT[:, m, :], in1=po[:, :], op=ALU.add))(m),
                 reads=[rpo, r_xT[m]], writes=[r_xT[m]])

    def store_out(ti):
        if final_norm:
            ps, rp = big.get()
            for kc in range(8):
                sq, rs = f512.get()
                P.op("act", (lambda kc: lambda e: e.activation(out=sq[:, :], in_=xT[:, kc, :], func=AF.Square))(kc), reads=[r_xT[kc]], writes=[rs])
                P.op("pe", (lambda kc: lambda e: e.matmul(ps[:, :], lhsT=ones_f[:], rhs=sq[:, :], start=(kc == 0), stop=(kc == 7)))(kc),
                     reads=[rs, r_const], writes=[rp])
            P.op("act", lambda e: e.activation(out=rstd_t[:, :], in_=ps[:, :], func=AF.Sqrt, scale=1.0 / D, bias=RMS_EPS), reads=[rp], writes=[r_rstd])
            P.op("dve", lambda e: e.reciprocal(out=rstd_t[:, :], in_=rstd_t[:, :]), reads=[r_rstd], writes=[r_rstd])
            for kc in range(8):
                P.op("dve", (lambda kc: lambda e: e.scalar_tensor_tensor(out=xT[:, kc, :], in0=xT[:, kc, :], scalar=cols[:, C_GF + kc:C_GF + kc + 1],
                                                                          in1=rstd_t[:, :], op0=ALU.mult, op1=ALU.mult))(kc),
                     reads=[r_xT[kc], r_rstd, r_cols], writes=[r_xT[kc]])
        for sbk in range(4):
            xo, rxo = xin.get()
            for half in range(2):
                pt, rp = big.get()
                for j in range(4):
                    kc = half * 4 + j
                    P.op("pe", (lambda kc, j: lambda e: e.transpose(out=pt[:, j * 128:(j + 1) * 128], in_=xT[:, kc, sbk * 128:(sbk + 1) * 128], identity=ident_f[:]))(kc, j),
                         reads=[r_xT[kc], r_const], writes=[rp])
                P.op("act", (lambda half: lambda e: e.activation(out=xo[:, half * 512:(half + 1) * 512], in_=pt[:, :], func=AF.Copy))(half), reads=[rp], writes=[rxo])
            r0 = ti * TW + sbk * 128
            P.dma("pool", "st_" + rxo.track, (lambda r0: lambda e: e.dma_start(out=y_d[r0:r0 + 128, :], in_=xo[:, :]))(r0), reads=[rxo])

    negA = sb("negA", [128, 16], F32); r_negA = Res("negA")

    def load_params(l):
        P.dma("sp", "cols", lambda e: e.dma_start(out=cols[:], in_=cols_d_l[l]), writes=[r_cols])
        P.dma("sp", "rows", lambda e: e.dma_start(out=rows[:], in_=rows_d_l[l].partition_broadcast(128)), writes=[r_rows])
        P.op("act", lambda e: e.activation(out=negA[:], in_=rows[:, R_ALOG:R_ALOG + 16], func=AF.Exp), reads=[r_rows], writes=[r_negA])
        P.op("dve", lambda e: e.tensor_scalar(out=negA[:], in0=negA[:], scalar1=-1.0, scalar2=None, op0=ALU.mult), reads=[r_negA], writes=[r_negA])

    def load_xT(src, r_src, ti):
        P.dma("pool", "xTld", lambda e: e.dma_start(out=xT[:, :, :], in_=src.rearrange("k p t -> p k t")[:, :, ti * TW:(ti + 1) * TW]),
              reads=[r_src[ti]], writes=r_xT)

    def store_xT(dst, r_dst, ti):
        P.dma("pool", "xTst", lambda e: e.dma_start(out=dst.rearrange("k p t -> p k t")[:, :, ti * TW:(ti + 1) * TW], in_=xT[:, :, :]),
              reads=r_xT, writes=[r_dst[ti]])

    def exchange(l):
        dece, rde = sm.get()
        P.op("act", lambda e: e.activation(out=dece[:, 0:20], in_=totacc[:, 0:20], func=AF.Exp), reads=[r_totacc], writes=[rde])
        for h in range(4):
            P.dma("pool", f"exp{l}", (lambda h: lambda e: e.dma_start(out=exp_d[l][:, h * 129:(h + 1) * 129], in_=CT[h][:]))(h), reads=[r_CT[h]], writes=[r_exp[l]])
        for gi in range(4):
            P.dma("pool", f"exp{l}", (lambda gi: lambda e: e.dma_start(out=exp_d[l][:, 516 + gi * 256:516 + (gi + 1) * 256], in_=HT[gi][:]))(gi),
                  reads=[r_HT[gi]], writes=[r_exp[l]])
        P.dma("pool", f"exp{l}", lambda e: e.dma_start(out=exp_d[l][:, 1540:1560], in_=dece[:, 0:20]), reads=[rde], writes=[r_exp[l]])
        P.dma("pool", f"cc{l}", lambda e: e.collective_compute("AllGather", ALU.bypass, replica_groups=[list(range(NCORES))],
                                                               ins=[exp_d[l]], outs=[gath_d[l]]),
              reads=[r_exp[l]], writes=[r_gath[l]], inc=1)
        for h in range(4):
            P.op("pool", (lambda h: lambda e: e.memset(CT[h][:], 0.0))(h), writes=[r_CT[h]])
            P.op("pool", (lambda h: lambda e: e.memset(HT[h][:], 0.0))(h), writes=[r_HT[h]])
        for r in range(NCORES - 1):
            dr, rdr = sm.get()
            P.dma("sp", rdr.track, (lambda r: lambda e: e.dma_start(out=dr[:, 0:20], in_=gath_d[l][r * 128:(r + 1) * 128, 1540:1560]))(r),
                  reads=[r_gath[l]], writes=[rdr])
            P.op("dve", (lambda r: lambda e: e.tensor_scalar(out=dr[:, 0:20], in0=dr[:, 0:20], scalar1=-1.0, scalar2=masks[:, r:r + 1], op0=ALU.add, op1=ALU.mult))(r),
                 reads=[rdr, r_masks], writes=[rdr])
            P.op("dve", lambda e: e.tensor_scalar(out=dr[:, 0:20], in0=dr[:, 0:20], scalar1=1.0, scalar2=None, op0=ALU.add), reads=[rdr], writes=[rdr])
            for h in range(4):
                tmp, rtmp = f256.get()
                P.dma("sp", rtmp.track, (lambda r, h: lambda e: e.dma_start(out=tmp[:, 0:129], in_=gath_d[l][r * 128:(r + 1) * 128, h * 129:(h + 1) * 129]))(r, h),
                      reads=[r_gath[l]], writes=[rtmp])
                P.op("dve", (lambda h: lambda e: e.tensor_scalar(out=CT[h][:], in0=CT[h][:], scalar1=dr[:, h:h + 1], scalar2=None, op0=ALU.mult))(h),
                     reads=[r_CT[h], rdr], writes=[r_CT[h]])
                P.op("dve", (lambda r, h: lambda e: e.scalar_tensor_tensor(out=CT[h][:], in0=tmp[:, 0:129], scalar=masks[:, r:r + 1], in1=CT[h][:], op0=ALU.mult, op1=ALU.add))(r, h),
                     reads=[r_CT[h], rtmp, r_masks], writes=[r_CT[h]])
            for gi in range(4):
                tmp, rtmp = f256.get()
                P.dma("sp", rtmp.track, (lambda r, gi: lambda e: e.dma_start(out=tmp[:, :], in_=gath_d[l][r * 128:(r + 1) * 128, 516 + gi * 256:516 + (gi + 1) * 256]))(r, gi),
                      reads=[r_gath[l]], writes=[rtmp])
                P.op("dve", (lambda gi: lambda e: e.tensor_tensor(out=HT[gi][:].rearrange("p (r q) -> p r q", r=4), in0=HT[gi][:].rearrange("p (r q) -> p r q", r=4),
                                                                   in1=dr[:, 4 + 4 * gi:8 + 4 * gi].unsqueeze(2).to_broadcast([128, 4, 64]), op=ALU.mult))(gi),
                     reads=[r_HT[gi], rdr], writes=[r_HT[gi]])
                P.op("dve", (lambda r, gi: lambda e: e.scalar_tensor_tensor(out=HT[gi][:], in0=tmp[:, :], scalar=masks[:, r:r + 1], in1=HT[gi][:], op0=ALU.mult, op1=ALU.add))(r, gi),
                     reads=[r_HT[gi], rtmp, r_masks], writes=[r_HT[gi]])
        for h in range(4):
            refresh_m(h); refresh_s(h)

    def export_tail():
        P.dma("pool", "tailx", lambda e: e.dma_start(out=tail_d.rearrange("p (k t) -> p k t", k=8), in_=xT[:, :, TW - HALO:TW]), reads=r_xT, writes=[r_tail])
        P.dma("pool", "cct", lambda e: e.collective_compute("AllGather", ALU.bypass, replica_groups=[list(range(NCORES))], ins=[tail_d], outs=[tailg_d]),
              reads=[r_tail], writes=[r_tailg], inc=1)

    def import_halo_x():
        for r in range(NCORES):
            tmp, rtmp = f256.get()
            P.dma("sp", rtmp.track, (lambda r: lambda e: e.dma_start(out=tmp[:, :], in_=tailg_d[r * 128:(r + 1) * 128, :]))(r), reads=[r_tailg], writes=[rtmp])
            tv = tmp[:, :].rearrange("p (k t) -> p k t", k=8)
            if r == 0:
                P.op("dve", (lambda tv: lambda e: e.tensor_scalar(out=xT[:, :, 0:HALO], in0=tv, scalar1=masks[:, 8:9], scalar2=None, op0=ALU.mult))(tv),
                     reads=[rtmp, r_masks], writes=r_xT)
            else:
                P.op("dve", (lambda r, tv: lambda e: e.scalar_tensor_tensor(out=xT[:, :, 0:HALO], in0=tv, scalar=masks[:, 8 + r:9 + r], in1=xT[:, :, 0:HALO],
                                                                              op0=ALU.mult, op1=ALU.add))(r, tv),
                     reads=[rtmp, r_masks] + r_xT, writes=r_xT)

    FILL_N = 3
    PRE_PANELS = set(range(PAN_K, PAN_O)) | set(range(PAN_XS, PAN_C))
    PRE_TOK = {0, 1, 6}
    PRE_TILES = set(range(4, 20))

    for l in range(2):
        w_bf = w_bf_l[l]; w_res = w_res_l[l]
        final_norm = (l == 1)
        load_params(l)
        if l == 0:
            load_x(0, HALO, 0)
        else:
            import_halo_x()
        ffn(HALO, "gu1", "d1", C_G1)
        rms_to_hbf(HALO, C_GM)
        inproj_feature(HALO, True)
        P.op("pool", lambda e: e.tensor_copy(out=halo4s[:, :, :], in_=halo4[:, :, :]), reads=r_h4, writes=[r_hsave])
        P.op("pool", lambda e: e.tensor_copy(out=uhs[:, :, :], in_=ubuf[:, :, 0:30]), reads=r_ub, writes=[r_hsave])
        zero_state()
        for ti in range(ntiles):
            if l == 0:
                for sbk in range(4):
                    load_x(HALO + ti * TW + sbk * 128, 128, sbk * 128)
            else:
                load_xT(xs1, r_xs1, ti)
            ffn(TW, "gu1", "d1", C_G1)
            store_xT(x1s, r_x1s, ti)
            rms_to_hbf(TW, C_GM)
            inproj_feature(TW, False, subset=PRE_PANELS)
            inproj_token(subset=PRE_TOK)
            G = gate_prep_tile()
            for c in range(4):
                g = gate_chunk(G, c)
                mlstm_chunk(c, g, pre=True)
                ssd_chunk(c, g, pre=True)
                tot, rtot = g["tot"]
                P.op("dve", (lambda tot: lambda e: e.tensor_tensor(out=totacc[:, :], in0=totacc[:, :], in1=tot[:, 0:20], op=ALU.add))(tot),
                     reads=[r_totacc, rtot], writes=[r_totacc])
        exchange(l)
        P.op("pool", lambda e: e.tensor_copy(out=halo4[:, :, :], in_=halo4s[:, :, :]), reads=[r_hsave], writes=r_h4)
        P.op("pool", lambda e: e.tensor_copy(out=ubuf[:, :, 0:30], in_=uhs[:, :, :]), reads=[r_hsave], writes=r_ub)
        for ti in range(ntiles):
            load_xT(x1s, r_x1s, ti)
            rms_to_hbf(TW, C_GM)
            inproj_feature(TW, False)
            inproj_token()
            G = gate_prep_tile()
            for c in range(4):
                g = gate_chunk(G, c)
                mlstm_chunk(c, g)
                ssd_chunk(c, g)
            conv_module()
            outproj()
            ffn(TW, "gu2", "d2", C_G2)
            if l == 0:
                store_xT(xs1, r_xs1, ti)
                if ti == ntiles - 1:
                    export_tail()
            else:
                store_out(ti)

    P.emit()
    return nc


def _gu(wg, wu):
    g = wg.reshape(8, 128, NF, 128).transpose(2, 1, 0, 3)
    u = wu.reshape(8, 128, NF, 128).transpose(2, 1, 0, 3)
    return np.ascontiguousarray(np.stack([g, u], axis=2).reshape(NF, 128, 2048))


def _dn(wd):
    return np.ascontiguousarray(wd.reshape(NF, 128, 8, 128).transpose(2, 1, 0, 3).reshape(8, 128, NF * 128))


def _panels(w):
    n = w.shape[1] // 128
    return w.reshape(8, 128, n, 128).transpose(2, 1, 0, 3).reshape(n, 128, 1024)


def layer_inputs(inp, l):
    w_in = inp["w_in"][l]
    q, k, v, o = w_in[:, 0:512], w_in[:, 512:1024], w_in[:, 1024:1536], w_in[:, 1536:2048]
    gi, gf = w_in[:, 2048:2052], w_in[:, 2052:2056]
    z = w_in[:, 2056:3080]
    xs, Bm, Cm = w_in[:, 3080:4104], w_in[:, 4104:4616], w_in[:, 4616:5128]
    dt = w_in[:, 5128:5144]
    ga, gb = w_in[:, 5144:5656], w_in[:, 5656:6168]
    gab = np.stack([ga.reshape(1024, 4, 128), gb.reshape(1024, 4, 128)], axis=2).reshape(1024, 1024)
    winf = np.ascontiguousarray(_panels(np.concatenate([q, k, o, xs, Bm, Cm, gab], axis=1)))
    small = np.concatenate([gi, gf, dt, np.zeros((1024, 256 - 24), np.float32)], axis=1)
    wtok = np.ascontiguousarray(np.stack([a.reshape(8, 128, 256).transpose(1, 0, 2).reshape(128, 2048)
                                          for a in (v[:, :256], v[:, 256:], z[:, :256], z[:, 256:512], z[:, 512:768], z[:, 768:], small)]))
    wout = np.ascontiguousarray(inp["w_out"][l].reshape(16, 128, 8, 128).transpose(2, 1, 0, 3).reshape(8, 128, 2048))
    col = lambda a: a.reshape(-1, 128).T
    cw4 = np.concatenate([inp["mlstm_conv_w"][l], inp["ssd_conv_w"][l]], axis=1)
    cb4 = np.concatenate([inp["mlstm_conv_b"][l], inp["ssd_conv_b"][l]])
    cw4c = cw4.reshape(4, 24, 128).transpose(2, 1, 0).reshape(128, 96)
    c31 = inp["cm_conv_w"][l].reshape(31, 4, 128).transpose(2, 1, 0).reshape(128, 124)
    cols = np.concatenate([col(inp["ffn1_norm"][l]), col(inp["mix_norm"][l]), col(inp["ffn2_norm"][l]), col(inp["final_norm"]),
                           cw4c, col(cb4), col(inp["mlstm_norm"][l]), c31, col(inp["cm_conv_b"][l]), col(inp["cm_ln_g"][l]),
                           col(inp["cm_ln_b"][l])], axis=1)
    cols = np.ascontiguousarray(np.concatenate([cols, np.zeros((128, 320 - cols.shape[1]), np.float32)], axis=1))
    rows = np.concatenate([inp["mlstm_gate_b"][l], inp["ssd_dt_bias"][l], inp["ssd_a_log"][l], inp["ssd_d"][l], inp["ssd_norm"][l]])
    rows = np.ascontiguousarray(np.concatenate([rows, np.zeros(1088 - rows.shape[0], np.float32)])[None, :])
    return {
        "gu1": _gu(inp["ffn1_w_gate"][l], inp["ffn1_w_up"][l]), "d1": _dn(inp["ffn1_w_down"][l]),
        "gu2": _gu(inp["ffn2_w_gate"][l], inp["ffn2_w_up"][l]), "d2": _dn(inp["ffn2_w_down"][l]),
        "winf": winf, "wtok": wtok, "wout": wout, "cols": cols, "rows": rows,
    }


_NC_CACHE = {}


def _get_nc(NT):
    if NT not in _NC_CACHE:
        _NC_CACHE[NT] = build_fused(NT)
    return _NC_CACHE[NT]


def kernel(**inputs):
    inp = {k: np.asarray(v, dtype=np.float32) for k, v in inputs.items()}
    x = inp["x"]
    B, S, _ = x.shape
    nseg = NCORES // B
    NT = S // nseg
    nc = _get_nc(NT)
    shared = {}
    for l in range(2):
        for k, v in layer_inputs(inp, l).items():
            shared[f"{k}_{l}"] = v
    in_maps = []
    for c in range(NCORES):
        b, sg = c // nseg, c % nseg
        halo = np.zeros((HALO, D), np.float32) if sg == 0 else x[b, sg * NT - HALO:sg * NT]
        masks = np.zeros((1, 16), np.float32)
        for r in range(NCORES):
            if r // nseg == b and r % nseg < sg:
                masks[0, r] = 1.0
        if sg > 0:
            masks[0, 8 + c - 1] = 1.0
        m = dict(shared)
        m["x"] = np.ascontiguousarray(np.concatenate([halo, x[b, sg * NT:(sg + 1) * NT]], axis=0))
        m["masks"] = masks
        in_maps.append(m)
    if os.environ.get("K_TRACE"):
        res = run_bass_kernel_spmd(nc, in_maps, core_ids=list(range(NCORES)), trace=True)
        print("EXEC_TIME_NS", res.exec_time_ns)
    else:
        res = run_bass_kernel_spmd(nc, in_maps, core_ids=list(range(NCORES)))
    out = np.empty_like(x)
    for c in range(NCORES):
        b, sg = c // nseg, c % nseg
        out[b, sg * NT:(sg + 1) * NT] = np.asarray(res.results[c]["y"])
    return out
```
